# Optimizing a Trainium2 kernel written in Bass

```python
import numpy as np
import jax, jax.numpy as jnp
from jax import lax

D_MODEL = 2048
BATCH = 4
SEQ = 8192
DEPTH = 4

CTX_LEN = 256
GRID_W = 64
MIX_WIDTH = D_MODEL
NA_WIDTH = D_MODEL // 2
NA_HEADS = 8
NA_HEAD_DIM = NA_WIDTH // NA_HEADS
CONV_WIDTH = MIX_WIDTH - NA_WIDTH
CONV_K = 3
WIN_R = 8
WIN_C = 16
D_FF = 4 * D_MODEL
N_MOD = 6
IN_SPLITS = tuple(int(s) for s in np.cumsum([NA_WIDTH, NA_WIDTH, NA_WIDTH, CONV_WIDTH, CONV_WIDTH]))
IN_COLS = 3 * NA_WIDTH + 3 * CONV_WIDTH
EPS = 1e-6

kernel_name = "hymba_natten_shortconv_dit_block"


def rms_norm(x, g):
    xf = x.astype(jnp.float32)
    y = xf * lax.rsqrt(jnp.mean(xf * xf, axis=-1, keepdims=True) + EPS)
    return (y * g.astype(jnp.float32)).astype(x.dtype)


def modulate(h, shift, scale):
    return h * (1 + scale) + shift


def to_heads(t):
    b, l, _ = t.shape
    return t.reshape(b, l, NA_HEADS, NA_HEAD_DIM).transpose(0, 2, 1, 3)


def from_heads(t):
    b, h, l, d = t.shape
    return t.transpose(0, 2, 1, 3).reshape(b, l, h * d)


def short_conv(u, w):
    l = u.shape[1]
    half = CONV_K // 2
    up = jnp.pad(u, ((0, 0), (half, half), (0, 0)))
    return sum(up[:, k:k + l] * w[k] for k in range(CONV_K))


def conv_mixer(b_gate, c_gate, v, w):
    return b_gate * short_conv(c_gate * v, w)


def context_attention(q, k, v):
    s = jnp.einsum('bhqd,bhkd->bhqk', q, k).astype(jnp.float32) * (NA_HEAD_DIM ** -0.5)
    p = jax.nn.softmax(s, axis=-1).astype(v.dtype)
    return jnp.einsum('bhqk,bhkd->bhqd', p, v)


def neighbourhood_attention(q, k, v, k_ctx, v_ctx, rpb):
    b, h, s, dh = q.shape
    rows = s // GRID_W
    kr = min(WIN_R, rows)
    kc = WIN_C
    kg = k.reshape(b, h, rows, GRID_W, dh)
    vg = v.reshape(b, h, rows, GRID_W, dh)
    q_rows = q.reshape(b, h, rows, GRID_W, dh).transpose(2, 0, 1, 3, 4)
    col = np.arange(GRID_W)
    col_start = np.clip(col - kc // 2, 0, GRID_W - kc)
    col_idx = col_start[:, None] + np.arange(kc)[None, :]
    dc = col_idx - col[:, None] + (WIN_C - 1)
    scale = NA_HEAD_DIM ** -0.5
    n_loc = kr * kc

    def one_row(args):
        r, q_r = args
        rs = jnp.clip(r - kr // 2, 0, rows - kr)
        k_rows = lax.dynamic_slice_in_dim(kg, rs, kr, axis=2)
        v_rows = lax.dynamic_slice_in_dim(vg, rs, kr, axis=2)
        k_win = k_rows[:, :, :, col_idx]
        v_win = v_rows[:, :, :, col_idx]
        s_loc = jnp.einsum('bhqd,bhaqcd->bhqac', q_r, k_win).astype(jnp.float32) * scale
        dr = rs + jnp.arange(kr) - r + (WIN_R - 1)
        bias = rpb[:, dr[:, None, None], dc[None, :, :]]
        s_loc = s_loc + bias.transpose(0, 2, 1, 3)[None].astype(jnp.float32)
        s_ctx = jnp.einsum('bhqd,bhkd->bhqk', q_r, k_ctx).astype(jnp.float32) * scale
        sc = jnp.concatenate([s_loc.reshape(b, h, GRID_W, n_loc), s_ctx], axis=-1)
        p = jax.nn.softmax(sc, axis=-1).astype(v.dtype)
        p_loc = p[..., :n_loc].reshape(b, h, GRID_W, kr, kc)
        p_ctx = p[..., n_loc:]
        return (jnp.einsum('bhqac,bhaqcd->bhqd', p_loc, v_win)
                + jnp.einsum('bhqk,bhkd->bhqd', p_ctx, v_ctx))

    out = lax.map(one_row, (jnp.arange(rows), q_rows))
    return out.transpose(1, 2, 0, 3, 4).reshape(b, h, s, dh)


def merge_and_project(o_na, o_conv, g_na, g_conv, w_o):
    mixed = jnp.concatenate([rms_norm(o_na, g_na), rms_norm(o_conv, g_conv)], axis=-1)
    return mixed @ w_o


def sq_relu_mlp(h, w1, w2):
    return jnp.square(jax.nn.relu(h @ w1)) @ w2


def setup_inputs(seed: int = 0) -> dict:
    key = jax.random.key(seed)
    ks = jax.random.split(key, 17)
    f32 = jnp.float32
    nrm = lambda k, shape, s: jax.random.normal(k, shape, f32) * s
    return {
        "x": nrm(ks[0], (BATCH, SEQ, D_MODEL), 1.0),
        "c": nrm(ks[1], (BATCH, D_MODEL), 1.0),
        "ctx": nrm(ks[2], (BATCH, CTX_LEN, D_MODEL), 1.0),
        "c_ctx": nrm(ks[3], (D_MODEL,), 1.0),
        "w_ada": nrm(ks[4], (DEPTH, D_MODEL, N_MOD * D_MODEL), D_MODEL ** -0.5),
        "b_ada": nrm(ks[5], (DEPTH, N_MOD * D_MODEL), 0.02),
        "g_norm1": 1.0 + nrm(ks[6], (DEPTH, D_MODEL), 0.05),
        "w_in": nrm(ks[7], (DEPTH, D_MODEL, IN_COLS), D_MODEL ** -0.5),
        "rpb": nrm(ks[8], (DEPTH, NA_HEADS, 2 * WIN_R - 1, 2 * WIN_C - 1), 0.1),
        "conv_w": nrm(ks[9], (DEPTH, CONV_K, CONV_WIDTH), CONV_K ** -0.5),
        "g_na_out": 1.0 + nrm(ks[10], (DEPTH, NA_WIDTH), 0.05),
        "g_conv_out": 1.0 + nrm(ks[11], (DEPTH, CONV_WIDTH), 0.05),
        "w_out": nrm(ks[12], (DEPTH, MIX_WIDTH, D_MODEL), MIX_WIDTH ** -0.5),
        "g_norm2": 1.0 + nrm(ks[13], (DEPTH, D_MODEL), 0.05),
        "w_mlp1": nrm(ks[14], (DEPTH, D_MODEL, D_FF), D_MODEL ** -0.5),
        "w_mlp2": nrm(ks[15], (DEPTH, D_FF, D_MODEL), D_FF ** -0.5),
        "g_final": 1.0 + nrm(ks[16], (D_MODEL,), 0.05),
    }


def reference(x, c, ctx, c_ctx, w_ada, b_ada, g_norm1, w_in, rpb, conv_w, g_na_out, g_conv_out,
              w_out, g_norm2, w_mlp1, w_mlp2, g_final):
    s_lat = jax.nn.silu(c)
    s_ctx = jax.nn.silu(c_ctx)
    for i in range(DEPTH):
        last = i == DEPTH - 1
        m_lat = (s_lat @ w_ada[i] + b_ada[i])[:, None, :]
        sh1, sc1, gt1, sh2, sc2, gt2 = jnp.split(m_lat, N_MOD, axis=-1)
        m_ctx = s_ctx @ w_ada[i] + b_ada[i]
        csh1, csc1, cgt1, csh2, csc2, cgt2 = jnp.split(m_ctx, N_MOD, axis=-1)

        h_lat = modulate(rms_norm(x, g_norm1[i]), sh1, sc1)
        h_ctx = modulate(rms_norm(ctx, g_norm1[i]), csh1, csc1)
        q, k, v, bg, cg, u = jnp.split(h_lat @ w_in[i], IN_SPLITS, axis=-1)
        qc, kc, vc, bgc, cgc, uc = jnp.split(h_ctx @ w_in[i], IN_SPLITS, axis=-1)
        k_ctx_h, v_ctx_h = to_heads(kc), to_heads(vc)

        o_na = from_heads(neighbourhood_attention(to_heads(q), to_heads(k), to_heads(v),
                                                  k_ctx_h, v_ctx_h, rpb[i]))
        o_conv = conv_mixer(bg, cg, u, conv_w[i])
        x_new = x + gt1 * merge_and_project(o_na, o_conv, g_na_out[i], g_conv_out[i], w_out[i])

        if not last:
            oc_na = from_heads(context_attention(to_heads(qc), k_ctx_h, v_ctx_h))
            oc_conv = conv_mixer(bgc, cgc, uc, conv_w[i])
            ctx = ctx + cgt1 * merge_and_project(oc_na, oc_conv, g_na_out[i], g_conv_out[i], w_out[i])
            hc2 = modulate(rms_norm(ctx, g_norm2[i]), csh2, csc2)
            ctx = ctx + cgt2 * sq_relu_mlp(hc2, w_mlp1[i], w_mlp2[i])

        h2 = modulate(rms_norm(x_new, g_norm2[i]), sh2, sc2)
        x = x_new + gt2 * sq_relu_mlp(h2, w_mlp1[i], w_mlp2[i])
    return rms_norm(x, g_final)
```

```python
import numpy as np
from contextlib import ExitStack
import concourse.bass as bass
import concourse.mybir as mybir
from concourse.bass_utils import run_bass_kernel_spmd

F32 = mybir.dt.float32
BF16 = mybir.dt.bfloat16
AF = mybir.ActivationFunctionType
ALU = mybir.AluOpType
ENGS = ["sync", "gpsimd", "scalar", "vector", "tensor"]

D = 2048
KC = 16
SEQ = 8192
OWN = 4096
NT = 5120
CTXL = 256
DEPTH = 4
NU = 192
U_Q, U_K, U_BG, U_CG, U_U, U_V, U_O, U_M1, U_M2 = 0, 8, 16, 24, 32, 40, 48, 64, 128
NEG = -30000.0
EPS = 1e-6


class Buf:
    def __init__(self, name=""):
        self.name = name
        self.w = None
        self.r = []


class DSem:
    def __init__(self, h, key):
        self.h = h
        self.key = key
        self.count = 0
        self.last = None
        self.nobarrier = False


class Prog:
    def __init__(self, nc, es):
        self.nc = nc
        self.es = es
        self.ops = {e: [] for e in ENGS}
        self.cnt = {e: 0 for e in ENGS}
        self.waited = {e: {} for e in ENGS}
        self.psem = {e: es.enter_context(nc.semaphore("p_" + e)) for e in ENGS}
        self.dsems = []
        self.free = []
        self.phase = []

    def dsem(self, persistent=False):
        if not persistent and self.free:
            s = self.free.pop()
        else:
            n = len(self.dsems)
            s = DSem(self.es.enter_context(self.nc.semaphore("d%d" % n)), "d%d" % n)
            self.dsems.append(s)
        if not persistent:
            self.phase.append(s)
        return s

    def end_phase(self):
        self.barrier()
        self.free.extend(self.phase)
        self.phase = []

    def op(self, eng, fn, reads=(), writes=(), dsem=None):
        waits = []
        for b in reads:
            if b.w is not None:
                waits.append(b.w)
        for b in writes:
            if b.w is not None:
                waits.append(b.w)
            waits.extend(b.r)
        if dsem is not None:
            if dsem.last is not None:
                waits.append(dsem.last)
            dsem.count += 16
            t = (dsem.h, dsem.count, dsem.key)
            dsem.last = t
            inc = (dsem.h, 16)
        else:
            self.cnt[eng] += 1
            t = (self.psem[eng], self.cnt[eng], eng)
            inc = (self.psem[eng], 1)
        self._emit(eng, waits, fn, inc)
        for b in reads:
            b.r.append(t)
            if len(b.r) > 64:
                b.r = b.r[-64:] if False else b.r
        for b in writes:
            b.w = t
            b.r = []
        return t

    def _emit(self, eng, waits, fn, inc):
        best = {}
        for (h, v, key) in waits:
            if key == eng and eng == "tensor":
                continue
            if v <= self.waited[eng].get(key, 0):
                continue
            if key not in best or best[key][1] < v:
                best[key] = (h, v)
        ws = []
        for key, (h, v) in best.items():
            self.waited[eng][key] = v
            ws.append((h, v))

        def run(e, ws=ws, fn=fn, inc=inc):
            for (h, v) in ws:
                e.wait_ge(h, v)
            if fn is not None:
                ins = fn(e)
                if inc is not None:
                    ins.then_inc(inc[0], inc[1])
        self.ops[eng].append(run)

    def barrier(self):
        ts = [(self.psem[e], self.cnt[e], e) for e in ENGS if self.cnt[e] > 0]
        ts += [s.last for s in self.dsems if s.last is not None and not s.nobarrier]
        for e in ENGS:
            self._emit(e, [t for t in ts if t[2] != e], None, None)

    def emit(self):
        with self.nc.Block() as block:
            @block.sync
            def _(e):
                for f in self.ops["sync"]:
                    f(e)

            @block.gpsimd
            def _(e):
                for f in self.ops["gpsimd"]:
                    f(e)

            @block.scalar
            def _(e):
                for f in self.ops["scalar"]:
                    f(e)

            @block.vector
            def _(e):
                for f in self.ops["vector"]:
                    f(e)

            @block.tensor
            def _(e):
                for f in self.ops["tensor"]:
                    f(e)


_UNIQ = [0]


class Rot:
    def __init__(self, P, es, nc, name, n, shape, dt, psum=False, sem=True):
        mk = nc.psum_tensor if psum else nc.sbuf_tensor
        _UNIQ[0] += 1
        self.t = [es.enter_context(mk("%s%d_%d" % (name, i, _UNIQ[0]), shape, dt)) for i in range(n)]
        self.b = [Buf("%s%d" % (name, i)) for i in range(n)]
        self.s = [P.dsem() for _ in range(n)] if (sem and not psum) else [None] * n
        self.i = 0
        self.n = n

    def next(self):
        i = self.i
        self.i = (self.i + 1) % self.n
        return self.t[i], self.b[i], self.s[i]


def n_in(l):
    return OWN + 256 * (4 - l)


def n_out(l):
    return OWN + 256 * (3 - l)


def tiles_of(n):
    out = []
    t = 0
    while t < n:
        w = min(512, n - t)
        out.append((t, w))
        t += w
    return out


def build(depth=DEPTH, dbg=False):
    NL = depth if dbg else DEPTH
    nc = bass.Bass("TRN2", target_bir_lowering=False)
    _osb, _ops = nc.sbuf_tensor, nc.psum_tensor

    def _sb(name, *a, **k):
        _UNIQ[0] += 1
        return _osb("%s_%d" % (name, _UNIQ[0]), *a, **k)

    def _ps(name, *a, **k):
        _UNIQ[0] += 1
        return _ops("%s_%d" % (name, _UNIQ[0]), *a, **k)

    def din(name, shape, dt=F32):
        return nc.dram_tensor(name, shape, dt, kind="ExternalInput").ap()

    def dscr(name, shape, dt):
        return nc.dram_tensor(name, shape, dt).ap()

    xin = din("xT", [KC, 128, NT])
    cin = din("ctxT", [KC, 128, CTXL])
    cc = din("cc", [128, KC, 2])
    w_ada = din("w_ada", [NL, D, 6 * D])
    b_ada = din("b_ada", [128, DEPTH, 96])
    g1 = din("g1", [128, DEPTH, KC])
    g2 = din("g2", [128, DEPTH, KC])
    gf = din("gf", [128, KC])
    gna = din("gna", [128, DEPTH, 8])
    gcv = din("gcv", [128, DEPTH, 8])
    cw = din("cw", [128, DEPTH, 8, 3])
    idin = din("ident", [128, 128])
    wts = din("wts", [NL, NU, 128, 2048])
    bias = din("bias", [NL, 8, 128, 4096])
    blk = din("blk", [NL, 8, 128, 704])
    outT = nc.dram_tensor("outT", [KC, 128, OWN], F32, kind="ExternalOutput").ap()

    wtb = [dscr("wtb%d" % l_, [NU, 128, 2048], BF16) for l_ in range(DEPTH)]
    biasb = [dscr("biasb%d" % l_, [8, 128, 4096], BF16) for l_ in range(DEPTH)]
    blkb = [dscr("blkb%d" % l_, [8, 128, 704], BF16) for l_ in range(DEPTH)]

    class Stream:
        pass

    def mkstream(name, n, col):
        s = Stream()
        s.name = name
        s.col = col
        s.n = n
        if dbg and name == "m":
            s.x = nc.dram_tensor("dbg_x", [KC, 128, n], F32, kind="ExternalOutput").ap()
        else:
            s.x = dscr(name + "_x", [KC, 128, n], F32)
        mk = dscr
        if dbg and name == "m":
            mk = lambda nm, sh, dt: nc.dram_tensor("dbg_" + nm, sh, dt, kind="ExternalOutput").ap()
        s.q = mk(name + "_q", [8, 128, n], BF16)
        s.k = mk(name + "_k", [8, 128, n], BF16)
        s.v = mk(name + "_v", [n, 1024], BF16)
        s.z = mk(name + "_z", [8, 128, n + 2], F32)
        s.bg = mk(name + "_bg", [8, 128, n], F32)
        s.h2 = mk(name + "_h2", [KC, 128, n], BF16)
        if dbg and name == "m":
            s.dmix = mk("mix", [KC, 128, n], BF16)
            s.dxn = mk("xn", [KC, 128, n], F32)
        s.b = {}
        return s

    def dbufs(s, what, cs, t0, n):
        out = []
        for c in cs:
            for t in range((t0 // 128) * 128, t0 + n, 128):
                key = (what, c, t // 128)
                if key not in s.b:
                    s.b[key] = Buf("%s_%s_%d_%d" % (s.name, what, c, t))
                out.append(s.b[key])
        return out

    R8 = list(range(8))
    R16 = list(range(16))
    SM = mkstream("m", NT, 0)
    SC = mkstream("c", CTXL, 1)
    b_wtb = [[Buf() for _ in range(NU)] for _ in range(DEPTH)]
    b_biasb = [[Buf() for _ in range(8)] for _ in range(DEPTH)]
    b_blkb = [[Buf() for _ in range(8)] for _ in range(DEPTH)]

    with ExitStack() as es:
        P = Prog(nc, es)
        def sb(name, shape, dt=F32):
            return es.enter_context(_sb(name, shape, dt))
        mods = sb("mods", [128, DEPTH, 96, 2])
        A1 = sb("A1", [128, DEPTH, KC, 2])
        A2 = sb("A2", [128, DEPTH, KC, 2])
        g1s = sb("g1s", [128, DEPTH, KC])
        g2s = sb("g2s", [128, DEPTH, KC])
        gfs = sb("gfs", [128, KC])
        gnas = sb("gnas", [128, DEPTH, 8])
        gcvs = sb("gcvs", [128, DEPTH, 8])
        cws = sb("cws", [128, DEPTH, 8, 3])
        bas = sb("bas", [128, DEPTH, 96])
        ones = sb("ones", [128, 128], BF16)
        ident = sb("ident_b", [128, 128], BF16)
        identf = sb("identf", [128, 128])
        epst = sb("epst", [128, 1])
        zpad = sb("zpad", [128, 8, 1])
        b_const = Buf("const")
        s_const = P.dsem(persistent=True)
        s_cast = [P.dsem(persistent=True) for _ in range(8)]
        for s_ in s_cast:
            s_.nobarrier = True
        castn = [0]

        def cast_dma(out_ap, in_ap, wbuf):
            s = s_cast[castn[0] % 8]
            castn[0] += 1
            P.op("gpsimd", lambda e: e.dma_start(out=out_ap, in_=in_ap, max_dma_last_dim=8192),
                 writes=[wbuf], dsem=s)

        def cast_jobs(l):
            jobs = []
            order = (list(range(0, 48)) + list(range(48, 192)))
            for u in order:
                jobs.append((wtb[l][u], wts[l, u], b_wtb[l][u]))
                if u == 47:
                    for h in range(8):
                        jobs.append((biasb[l][h], bias[l, h], b_biasb[l][h]))
                        jobs.append((blkb[l][h], blk[l, h], b_blkb[l][h]))
            return jobs

        pending = []

        def pump(n):
            for _ in range(n):
                if pending:
                    cast_dma(*pending.pop(0))

        for (dst, src) in [(g1s, g1), (g2s, g2), (gfs, gf), (gnas, gna), (gcvs, gcv), (cws, cw), (bas, b_ada),
                           (identf, idin)]:
            P.op("sync", lambda e, dst=dst, src=src: e.dma_start(out=dst[:], in_=src), writes=[b_const], dsem=s_const)
        P.op("vector", lambda e: e.memset(ones[:], 1.0), writes=[b_const])
        P.op("vector", lambda e: e.memset(epst[:], EPS), writes=[b_const])
        P.op("vector", lambda e: e.memset(zpad[:], 0.0), writes=[b_const])
        P.op("vector", lambda e: e.tensor_copy(out=ident[:], in_=identf[:]), reads=[b_const], writes=[b_const])
        for s in (SM, SC):
            for col in (0, s.n + 1):
                P.op("gpsimd", lambda e, s=s, col=col: e.dma_start(
                    out=s.z[:, :, col:col + 1].rearrange("c p t -> p c t"), in_=zpad[:], allow_slow_non_contiguous=True),
                    reads=[b_const], writes=[Buf()], dsem=s_const)
        s_init = [P.dsem() for _ in range(4)]
        for (s, src) in ((SM, xin), (SC, cin)):
            for c in range(KC):
                P.op("gpsimd", lambda e, s=s, src=src, c=c: e.dma_start(out=s.x[c], in_=src[c]),
                     writes=[Buf()], dsem=s_init[c % 4])
        pending.extend(cast_jobs(0))
        pump(len(pending))

        sl = sb("sl", [128, KC, 2])
        sls = sb("sls", [128, KC, 2])
        slsb = sb("slsb", [128, KC, 2], BF16)
        b_sl = Buf()
        b_mods = [Buf() for _ in range(DEPTH)]

        def mods_steps(l, wa, pm, b_pm):
            steps = []
            for fg in range(48):
                def mk(fg=fg):
                    st = {}

                    def load():
                        t, b, s = wa.next()
                        st["t"], st["b"] = t, b
                        P.op("gpsimd", lambda e, t=t: e.dma_start(
                            out=t[:], in_=w_ada[l, :, fg * 256:(fg + 1) * 256].rearrange("(k p) n -> p k n", p=128)),
                            writes=[b], dsem=s)

                    def comp():
                        t, b = st["t"], st["b"]

                        def mm(e, t=t):
                            ins = None
                            for f2 in range(2):
                                for k in range(KC):
                                    ins = e.matmul(pm[:, fg * 2 + f2, :], lhsT=t[:, k, f2 * 128:(f2 + 1) * 128], rhs=slsb[:, k, :],
                                                   start=(k == 0), stop=(k == KC - 1))
                            return ins
                        P.op("tensor", mm, reads=[b, b_sl], writes=[b_pm])
                    return load, comp
                steps.append(mk())

            def fin():
                for col in range(2):
                    P.op("vector", lambda e, col=col: e.tensor_tensor(
                        out=mods[:, l, :, col], in0=pm[:, :, col], in1=bas[:, l, :], op=ALU.add),
                        reads=[b_pm, b_const], writes=[b_mods[l]])
                    for (A, g, off) in ((A1, g1s, 16), (A2, g2s, 64)):
                        P.op("vector", lambda e, col=col, A=A, g=g, off=off: e.scalar_tensor_tensor(
                            out=A[:, l, :, col], in0=mods[:, l, off:off + 16, col], scalar=1.0, in1=g[:, l, :],
                            op0=ALU.add, op1=ALU.mult), reads=[b_mods[l], b_const], writes=[b_mods[l]])
            steps.append((fin, None))
            return steps

        with ExitStack() as ps_:
            s_sl = P.dsem()
            wa = Rot(P, ps_, nc, "wa", 2, [128, KC, 256], BF16)
            pm = ps_.enter_context(_ps("pm", [128, 96, 2], F32))
            b_pm = Buf()
            P.op("sync", lambda e: e.dma_start(out=sl[:], in_=cc), writes=[b_sl], dsem=s_sl)
            P.op("scalar", lambda e: e.activation(out=sls[:], in_=sl[:], func=AF.Silu), reads=[b_sl], writes=[b_sl])
            P.op("vector", lambda e: e.tensor_copy(out=slsb[:], in_=sls[:]), reads=[b_sl], writes=[b_sl])
            for (ld_, cp_) in mods_steps(0, wa, pm, b_pm):
                ld_()
                if cp_ is not None:
                    cp_()
            if dbg:
                dmods = nc.dram_tensor("dbg_mods", [128, DEPTH * 96 * 2], F32, kind="ExternalOutput").ap()
                P.op("gpsimd", lambda e: e.dma_start(out=dmods, in_=mods[:].rearrange("p a b c -> p (a b c)")),
                     reads=[b_mods[0]], writes=[Buf()], dsem=s_sl)
                dbias = nc.dram_tensor("dbg_bias", [128, 4096], BF16, kind="ExternalOutput").ap()
                P.op("gpsimd", lambda e: e.dma_start(out=dbias, in_=biasb[0][3]),
                     reads=[b_biasb[0][3]], writes=[Buf()], dsem=s_sl)
            P.end_phase()

        def stats_rstd(sq_t, c0, nchunks, N, pbank, b_pbank, std_t, rstd_t, b_sq, b_rstd, inv_n):
            def mm(e):
                ins = None
                for c in range(nchunks):
                    ins = e.matmul(pbank[:, :N], lhsT=ones[:], rhs=sq_t[:, c0 + c, :N], start=(c == 0),
                                   stop=(c == nchunks - 1))
                return ins
            P.op("tensor", mm, reads=[b_sq, b_const], writes=[b_pbank])
            P.op("scalar", lambda e: e.activation(out=std_t[:, :N], in_=pbank[:, :N], func=AF.Sqrt, bias=epst[:, 0:1],
                                                  scale=inv_n), reads=[b_pbank, b_const], writes=[b_rstd])
            P.op("vector", lambda e: e.reciprocal(out=rstd_t[:, :N], in_=std_t[:, :N]), reads=[b_rstd], writes=[b_rstd])

        def layer(l):
            last = (l == DEPTH - 1)
            if l + 1 < depth:
                pending.extend(cast_jobs(l + 1))
            streams_A = [(SC, [(0, CTXL)]), (SM, tiles_of(n_in(l)))]
            streams_B = ([] if last else [(SC, [(0, CTXL)])]) + [(SM, tiles_of(n_out(l)))]
            def phaseA(pa):
                xts = [pa.enter_context(_sb("xt%d" % i_, [128, KC, 512], F32)) for i_ in range(2)]
                b_xts = [Buf(), Buf()]
                s_xts = [P.dsem(), P.dsem()]
                sq = pa.enter_context(_sb("sq", [128, KC, 512], BF16))
                b_sq = Buf()
                hts = [pa.enter_context(_sb("ht%d" % i_, [128, KC, 512], BF16)) for i_ in range(2)]
                b_hts = [Buf(), Buf()]
                cgt = pa.enter_context(_sb("cgt", [128, 8, 512], F32))
                b_cg = [Buf() for _ in range(8)]
                std_t = pa.enter_context(_sb("std", [128, 512], F32))
                rstd_t = pa.enter_context(_sb("rstd", [128, 512], F32))
                b_rstd = Buf()
                tmp = Rot(P, pa, nc, "tmp", 2, [128, 512], F32, sem=False)
                stb = Rot(P, pa, nc, "stb", 3, [128, 512], BF16)
                stf = Rot(P, pa, nc, "stf", 3, [128, 512], F32)
                wb = Rot(P, pa, nc, "wb", 3, [128, 2048], BF16)
                wv = Rot(P, pa, nc, "wv", 2, [128, 8192], BF16)
                pst = pa.enter_context(_ps("pst", [128, 512], F32))
                b_pst = Buf()
                pb = Rot(P, pa, nc, "pb", 4, [128, 512], F32, psum=True)
                def norm_ops(i, S, t0, N):
                    xt, b_xt, s_xt = xts[i % 2], b_xts[i % 2], s_xts[i % 2]
                    ht, b_ht = hts[i % 2], b_hts[i % 2]
                    col = S.col
                    ops = []

                    def load():
                        P.op("sync", lambda e: e.dma_start(
                            out=xt[:, :, :N], in_=S.x[:, :, t0:t0 + N].rearrange("c p t -> p c t")),
                            reads=dbufs(S, "x", R16, t0, N), writes=[b_xt], dsem=s_xt)
                    ops.append(load)
                    ops.extend([None, None, None])

                    def stats():
                        for c4 in range(4):
                            P.op("scalar", lambda e, c4=c4: e.activation(
                                out=sq[:, c4 * 4:(c4 + 1) * 4, :N], in_=xt[:, c4 * 4:(c4 + 1) * 4, :N], func=AF.Square),
                                reads=[b_xt], writes=[b_sq])
                        stats_rstd(sq, 0, KC, N, pst, b_pst, std_t, rstd_t, b_sq, b_rstd, 1.0 / D)
                    ops.append(stats)
                    for c in range(KC):
                        def chunk(c=c):
                            tt, tb, _ = tmp.next()
                            P.op("vector", lambda e: e.scalar_tensor_tensor(
                                out=tt[:, :N], in0=xt[:, c, :N], scalar=A1[:, l, c, col:col + 1], in1=rstd_t[:, :N],
                                op0=ALU.mult, op1=ALU.mult), reads=[b_xt, b_rstd, b_const, b_mods[l]], writes=[tb])
                            P.op("scalar", lambda e: e.activation(
                                out=ht[:, c, :N], in_=tt[:, :N], func=AF.Identity, bias=mods[:, l, c, col:col + 1], scale=1.0),
                                reads=[tb, b_const, b_mods[l]], writes=[b_ht])
                        ops.append(chunk)
                    return ops

                def inproj(S, t0, N, ht, b_ht, hook):
                    for (u0, kind) in ((U_Q, "q"), (U_K, "k"), (U_BG, "bg"), (U_CG, "cg"), (U_U, "u")):
                        for c in range(8):
                            wt_, wbuf, wsem = wb.next()
                            P.op("sync", lambda e, wt_=wt_, u=u0 + c: e.dma_start(out=wt_[:], in_=wtb[l][u]),
                                 reads=[b_wtb[l][u0 + c]], writes=[wbuf], dsem=wsem)
                            pt, pbuf, _ = pb.next()

                            def mm(e, wt_=wt_, pt=pt, N=N):
                                ins = None
                                for k in range(KC):
                                    ins = e.matmul(pt[:, :N], lhsT=wt_[:, k * 128:(k + 1) * 128], rhs=ht[:, k, :N],
                                                   start=(k == 0), stop=(k == KC - 1))
                                return ins
                            P.op("tensor", mm, reads=[wbuf, b_ht], writes=[pbuf])
                            hook()
                            if kind == "q":
                                st, sb_, ss = stb.next()
                                P.op("scalar", lambda e, st=st, pt=pt, N=N: e.activation(
                                    out=st[:, :N], in_=pt[:, :N], func=AF.Identity, scale=float(128 ** -0.5)),
                                    reads=[pbuf], writes=[sb_])
                                P.op("gpsimd", lambda e, st=st, S=S, c=c, t0=t0, N=N: e.dma_start(
                                    out=S.q[c, :, t0:t0 + N], in_=st[:, :N]), reads=[sb_],
                                    writes=dbufs(S, "q", [c], t0, N), dsem=ss)
                            elif kind == "k":
                                st, sb_, ss = stb.next()
                                P.op("vector", lambda e, st=st, pt=pt, N=N: e.tensor_copy(out=st[:, :N], in_=pt[:, :N]),
                                     reads=[pbuf], writes=[sb_])
                                P.op("gpsimd", lambda e, st=st, S=S, c=c, t0=t0, N=N: e.dma_start(
                                    out=S.k[c, :, t0:t0 + N], in_=st[:, :N]), reads=[sb_],
                                    writes=dbufs(S, "k", [c], t0, N), dsem=ss)
                            elif kind == "bg":
                                st, sb_, ss = stf.next()
                                P.op("scalar", lambda e, st=st, pt=pt, N=N: e.activation(
                                    out=st[:, :N], in_=pt[:, :N], func=AF.Copy), reads=[pbuf], writes=[sb_])
                                P.op("gpsimd", lambda e, st=st, S=S, c=c, t0=t0, N=N: e.dma_start(
                                    out=S.bg[c, :, t0:t0 + N], in_=st[:, :N]), reads=[sb_],
                                    writes=dbufs(S, "bg", [c], t0, N), dsem=ss)
                            elif kind == "cg":
                                P.op("vector", lambda e, c=c, pt=pt, N=N: e.tensor_copy(out=cgt[:, c, :N], in_=pt[:, :N]),
                                     reads=[pbuf], writes=[b_cg[c]])
                            else:
                                st, sb_, ss = stf.next()
                                P.op("vector", lambda e, st=st, pt=pt, c=c, N=N: e.tensor_tensor(
                                    out=st[:, :N], in0=pt[:, :N], in1=cgt[:, c, :N], op=ALU.mult),
                                    reads=[pbuf, b_cg[c]], writes=[sb_])
                                P.op("gpsimd", lambda e, st=st, S=S, c=c, t0=t0, N=N: e.dma_start(
                                    out=S.z[c, :, 1 + t0:1 + t0 + N], in_=st[:, :N]), reads=[sb_],
                                    writes=dbufs(S, "z", [c], t0, N), dsem=ss)
                    for g in range(2):
                        wt_, wbuf, wsem = wv.next()
                        P.op("sync", lambda e, wt_=wt_, g=g: e.dma_start(
                            out=wt_[:].rearrange("p (u e) -> p u e", u=4),
                            in_=wtb[l][U_V + 4 * g:U_V + 4 * g + 4].rearrange("u p e -> p u e")),
                            reads=b_wtb[l][U_V + 4 * g:U_V + 4 * g + 4], writes=[wbuf], dsem=wsem)
                        for tb_ in range(N // 128):
                            pt, pbuf, _ = pb.next()

                            def mm(e, wt_=wt_, pt=pt, tb_=tb_):
                                ins = None
                                for k in range(KC):
                                    ins = e.matmul(pt[:, :], lhsT=ht[:, k, tb_ * 128:(tb_ + 1) * 128],
                                                   rhs=wt_[:, k * 512:(k + 1) * 512], start=(k == 0), stop=(k == KC - 1))
                                return ins
                            P.op("tensor", mm, reads=[wbuf, b_ht], writes=[pbuf])
                            hook()
                            st, sb_, ss = stb.next()
                            if tb_ % 2 == 0:
                                P.op("scalar", lambda e, st=st, pt=pt: e.activation(out=st[:], in_=pt[:], func=AF.Copy),
                                     reads=[pbuf], writes=[sb_])
                            else:
                                P.op("vector", lambda e, st=st, pt=pt: e.tensor_copy(out=st[:], in_=pt[:]),
                                     reads=[pbuf], writes=[sb_])
                            P.op("gpsimd", lambda e, st=st, S=S, t0=t0, tb_=tb_, g=g: e.dma_start(
                                out=S.v[t0 + tb_ * 128:t0 + (tb_ + 1) * 128, g * 512:(g + 1) * 512], in_=st[:]),
                                reads=[sb_], writes=dbufs(S, "v", [g], t0 + tb_ * 128, 128), dsem=ss)

                tlistA = [(S, t0, N) for (S, tl) in streams_A for (t0, N) in tl]
                for op_ in norm_ops(0, *tlistA[0]):
                    if op_ is not None:
                        op_()
                for i_, (S, t0, N) in enumerate(tlistA):
                    bgq = norm_ops(i_ + 1, *tlistA[i_ + 1]) if i_ + 1 < len(tlistA) else []
                    cnt = [0]

                    def hook(bgq=bgq, cnt=cnt):
                        cnt[0] += 1
                        if bgq and cnt[0] % 2 == 0:
                            op_ = bgq.pop(0)
                            if op_ is not None:
                                op_()
                    inproj(S, t0, N, hts[i_ % 2], b_hts[i_ % 2], hook)
                    while bgq:
                        op_ = bgq.pop(0)
                        if op_ is not None:
                            op_()
                    pump(7)
                P.end_phase()

            def phaseB(pbk):
                def sbp(name, shape, dt=F32):
                    return pbk.enter_context(_sb(name, shape, dt))
                kctx = sbp("kctx", [128, 8, CTXL], BF16)
                vctx = sbp("vctx", [128, 2, 1024], BF16)
                b_ctxkv = Buf()
                s_ctxkv = [P.dsem(), P.dsem()]
                qt = sbp("qt", [128, 8, 512], BF16)
                b_qt = Buf()
                s_qt = P.dsem()
                kt = sbp("kt", [128, 8, 1024], BF16)
                b_kt = Buf()
                s_kt = P.dsem()
                vt = sbp("vt", [128, 8, 1024], BF16)
                b_vt = Buf()
                s_vt = P.dsem()
                bs = Rot(P, pbk, nc, "bs", 1, [128, 8, 512], BF16)
                blkt = sbp("blkt", [128, 8, 704], BF16)
                b_blk = Buf()
                s_blk = P.dsem()
                P.op("sync", lambda e: e.dma_start(out=blkt[:], in_=blkb[l].rearrange("h p e -> p h e")),
                     reads=b_blkb[l], writes=[b_blk], dsem=s_blk)
                ptl = Rot(P, pbk, nc, "ptl", 3, [128, 512], BF16, sem=False)
                oo = sbp("oo", [128, KC, 512], F32)
                b_oo = [Buf() for _ in range(KC)]
                sqb = sbp("sqb", [128, KC, 512], BF16)
                b_sqb = Buf()
                rden = Rot(P, pbk, nc, "rden", 2, [128, 512], F32, sem=False)
                zt = Rot(P, pbk, nc, "zt", 2, [128, 514], F32)
                bgt = Rot(P, pbk, nc, "bgt", 2, [128, 512], F32)
                acc = Rot(P, pbk, nc, "acc", 2, [128, 512], F32, sem=False)
                mix = sbp("mix", [128, KC, 512], BF16)
                b_mix = Buf()
                ocv = sbp("ocv", [128, 8, 512], BF16)
                b_ocv = [Buf() for _ in range(8)]
                sqc = sbp("sqc", [128, 8, 512], BF16)
                b_sqc = Buf()
                stdc_t = sbp("stdcv", [128, 512])
                rstdc_t = sbp("rstdcv", [128, 512])
                b_rstdc = Buf()
                xc = Rot(P, pbk, nc, "xcb", 2, [128, 512], F32)
                xs_sem = [P.dsem() for _ in range(4)]
                hst = Rot(P, pbk, nc, "hst", 2, [128, 512], BF16)
                std_t = sbp("stdb", [128, 512])
                rstd_t = sbp("rstdb", [128, 512])
                b_rstd = Buf()
                tmp = Rot(P, pbk, nc, "tmpb", 2, [128, 512], F32, sem=False)
                wb = Rot(P, pbk, nc, "wbb", 3, [128, 2048], BF16)
                psS = Rot(P, pbk, nc, "psS", 2, [128, 512], F32, psum=True)
                psO = Rot(P, pbk, nc, "psO", 2, [128, 512], F32, psum=True)
                psD = Rot(P, pbk, nc, "psD", 2, [128, 512], F32, psum=True)
                psG = Rot(P, pbk, nc, "psG", 2, [128, 512], F32, psum=True)
                P.op("sync", lambda e: e.dma_start(out=kctx[:], in_=SC.k.rearrange("h p t -> p h t")),
                     reads=dbufs(SC, "k", R8, 0, CTXL), writes=[b_ctxkv], dsem=s_ctxkv[0])
                P.op("sync", lambda e: e.dma_start(out=vctx[:], in_=SC.v.rearrange("(c p) f -> p c f", p=128)),
                     reads=dbufs(SC, "v", [0, 1], 0, CTXL), writes=[b_ctxkv], dsem=s_ctxkv[1])
                def emit_loads(S, t0, N):
                    isctx = (S is SC)
                    P.op("sync", lambda e, S=S, t0=t0, N=N: e.dma_start(
                        out=qt[:, :, :N], in_=S.q[:, :, t0:t0 + N].rearrange("h p t -> p h t")),
                        reads=dbufs(S, "q", R8, t0, N), writes=[b_qt], dsem=s_qt)
                    chunks = []
                    if not isctx:
                        kbase = t0 - 256
                        k0 = max(0, kbase)
                        k1 = min(n_in(l), t0 + (768 if N == 512 else 512))
                        nk = k1 - k0
                        P.op("sync", lambda e, k0=k0, nk=nk: e.dma_start(
                            out=kt[:, :, :nk], in_=SM.k[:, :, k0:k0 + nk].rearrange("h p t -> p h t")),
                            reads=dbufs(SM, "k", R8, k0, nk), writes=[b_kt], dsem=s_kt)
                        P.op("sync", lambda e, k0=k0, nk=nk: e.dma_start(
                            out=vt[:, :nk // 128, :], in_=SM.v[k0:k0 + nk, :].rearrange("(c p) f -> p c f", p=128)),
                            reads=dbufs(SM, "v", [0, 1], k0, nk), writes=[b_vt], dsem=s_vt)
                        for ci in range(nk // 128):
                            j = (k0 + ci * 128 - kbase) // 128
                            chunks.append(("loc", j, ci))
                    chunks += [("ctx", 0, 0), ("ctx", 1, 1)]
                    return chunks

                def emit_heads(S, t0, N, chunks, nxt_tile):
                    col = S.col
                    isctx = (S is SC)
                    for c in range(8):
                        P.op("vector", lambda e, c=c, N=N: e.scalar_tensor_tensor(
                            out=mix[:, 8 + c, :N], in0=ocv[:, c, :N], scalar=gcvs[:, l, c:c + 1], in1=rstdc_t[:, :N],
                            op0=ALU.mult, op1=ALU.mult), reads=[b_ocv[c], b_rstdc, b_const], writes=[b_mix])
                    edge = (not isctx) and t0 == 0
                    for h in range(8):
                        bt = bb = None
                        if edge:
                            bt, bb, bsem = bs.next()
                            P.op("sync", lambda e, bt=bt, h=h: e.dma_start(
                                out=bt[:].rearrange("p j q -> p (j q)"), in_=biasb[l][h]),
                                reads=[b_biasb[l][h]], writes=[bb], dsem=bsem)
                        pO, bO, _ = psO.next()
                        pD, bD, _ = psD.next()
                        def emitS(idx, h=h, bt=bt, bb=bb, N=N, edge=edge):
                            kind, j, ci = chunks[idx]
                            pS, bS, _ = psS.next()
                            if kind == "loc" and not edge:
                                def mm(e, pS=pS, h=h, ci=ci, j=j, N=N):
                                    e.matmul(pS[:, :N], lhsT=kt[:, h, ci * 128:(ci + 1) * 128], rhs=qt[:, h, :N],
                                             start=True, stop=False)
                                    na = N // 64
                                    ins = None
                                    for a in range(na):
                                        d0 = 2 * j - a
                                        bi = d0 + 1 if -1 <= d0 <= 8 else 10
                                        ins = e.matmul(pS[:, a * 64:(a + 1) * 64], lhsT=ident[:],
                                                       rhs=blkt[:, h, bi * 64:(bi + 1) * 64], start=False, stop=(a == na - 1))
                                    return ins
                                P.op("tensor", mm, reads=[b_kt, b_qt, b_blk, b_const], writes=[bS])
                            elif kind == "loc":
                                def mm(e, pS=pS, h=h, ci=ci, j=j, bt=bt, N=N):
                                    e.matmul(pS[:, :N], lhsT=kt[:, h, ci * 128:(ci + 1) * 128], rhs=qt[:, h, :N],
                                             start=True, stop=False)
                                    return e.matmul(pS[:, :N], lhsT=ident[:], rhs=bt[:, j, :N], start=False, stop=True)
                                P.op("tensor", mm, reads=[b_kt, b_qt, bb, b_const], writes=[bS])
                            else:
                                def mm(e, pS=pS, h=h, j=j, N=N):
                                    return e.matmul(pS[:, :N], lhsT=kctx[:, h, j * 128:(j + 1) * 128], rhs=qt[:, h, :N],
                                                    start=True, stop=True)
                                P.op("tensor", mm, reads=[b_ctxkv, b_qt], writes=[bS])
                            return pS, bS
                        nch = len(chunks)
                        nxt = emitS(0)
                        for idx, (kind, j, ci) in enumerate(chunks):
                            pS, bS = nxt
                            if idx + 1 < nch:
                                nxt = emitS(idx + 1)
                            vl = (vt, ci, b_vt) if kind == "loc" else (vctx, ci, b_ctxkv)
                            pt_, pbuf_, _ = ptl.next()
                            P.op("scalar", lambda e, pt_=pt_, pS=pS, N=N: e.activation(
                                out=pt_[:, :N], in_=pS[:, :N], func=AF.Exp), reads=[bS], writes=[pbuf_])
                            first = (idx == 0)
                            lastc = (idx == nch - 1)

                            def mm2(e, pt_=pt_, vl=vl, h=h, first=first, lastc=lastc, pO=pO, pD=pD, N=N):
                                e.matmul(pO[:, :N], lhsT=vl[0][:, vl[1], h * 128:(h + 1) * 128], rhs=pt_[:, :N],
                                         start=first, stop=lastc)
                                return e.matmul(pD[:, :N], lhsT=ones[:], rhs=pt_[:, :N], start=first, stop=lastc)
                            P.op("tensor", mm2, reads=[pbuf_, vl[2], b_const], writes=[bO, bD])
                        rt, rb, _ = rden.next()
                        P.op("vector", lambda e, rt=rt, pD=pD, N=N: e.reciprocal(out=rt[:, :N], in_=pD[:, :N]),
                             reads=[bD], writes=[rb])
                        P.op("vector", lambda e, rt=rt, pO=pO, h=h, N=N: e.tensor_tensor(
                            out=oo[:, h, :N], in0=pO[:, :N], in1=rt[:, :N], op=ALU.mult),
                            reads=[bO, rb], writes=[b_oo[h]])
                        P.op("vector", lambda e, h=h, N=N: e.tensor_tensor(out=sqb[:, h, :N], in0=oo[:, h, :N], in1=oo[:, h, :N],
                                                                        op=ALU.mult),
                             reads=[b_oo[h]], writes=[b_sqb])
                        if nxt_tile is not None:
                            if h == 0:
                                conv_load(nxt_tile[0], nxt_tile[1], nxt_tile[2], 0)
                            if h + 1 < 8:
                                conv_load(nxt_tile[0], nxt_tile[1], nxt_tile[2], h + 1)
                            conv_chunk(nxt_tile[0], nxt_tile[1], nxt_tile[2], h)
                    if nxt_tile is not None:
                        conv_stats(nxt_tile[2])

                cv_pending = {}

                def conv_load(S, t0, N, c):
                    z_, zb, zs = zt.next()
                    lo = max(0, t0 - 1)
                    hi = min(S.n, t0 + N + 1)
                    P.op("sync", lambda e, z_=z_, S=S, c=c, t0=t0, N=N: e.dma_start(
                        out=z_[:, :N + 2], in_=S.z[c, :, t0:t0 + N + 2]),
                        reads=dbufs(S, "z", [c], lo, hi - lo), writes=[zb], dsem=zs)
                    g_, gb, gs = bgt.next()
                    P.op("sync", lambda e, g_=g_, S=S, c=c, t0=t0, N=N: e.dma_start(
                        out=g_[:, :N], in_=S.bg[c, :, t0:t0 + N]), reads=dbufs(S, "bg", [c], t0, N),
                        writes=[gb], dsem=gs)
                    cv_pending[c] = (z_, zb, g_, gb)

                def conv_chunk(S, t0, N, c):
                    z_, zb, g_, gb = cv_pending.pop(c)
                    a_, ab, _ = acc.next()
                    P.op("vector", lambda e, a_=a_, z_=z_, c=c, N=N: e.tensor_scalar(
                        out=a_[:, :N], in0=z_[:, 1:N + 1], scalar1=cws[:, l, c, 1:2], scalar2=None, op0=ALU.mult),
                        reads=[zb, b_const], writes=[ab])
                    P.op("vector", lambda e, a_=a_, z_=z_, c=c, N=N: e.scalar_tensor_tensor(
                        out=a_[:, :N], in0=z_[:, 0:N], scalar=cws[:, l, c, 0:1], in1=a_[:, :N],
                        op0=ALU.mult, op1=ALU.add), reads=[zb, ab, b_const], writes=[ab])
                    P.op("vector", lambda e, a_=a_, z_=z_, c=c, N=N: e.scalar_tensor_tensor(
                        out=a_[:, :N], in0=z_[:, 2:N + 2], scalar=cws[:, l, c, 2:3], in1=a_[:, :N],
                        op0=ALU.mult, op1=ALU.add), reads=[zb, ab, b_const], writes=[ab])
                    P.op("vector", lambda e, a_=a_, g_=g_, N=N: e.tensor_tensor(
                        out=a_[:, :N], in0=a_[:, :N], in1=g_[:, :N], op=ALU.mult),
                        reads=[ab, gb], writes=[ab])
                    P.op("vector", lambda e, a_=a_, c=c, N=N: e.tensor_copy(out=ocv[:, c, :N], in_=a_[:, :N]),
                         reads=[ab], writes=[b_ocv[c]])
                    P.op("vector", lambda e, a_=a_, c=c, N=N: e.tensor_tensor(out=sqc[:, c, :N], in0=a_[:, :N], in1=a_[:, :N],
                                                                            op=ALU.mult),
                         reads=[ab], writes=[b_sqc])

                def conv_stats(N):
                    pG, bG, _ = psG.next()
                    stats_rstd(sqc, 0, 8, N, pG, bG, stdc_t, rstdc_t, b_sqc, b_rstdc, 1.0 / 1024)

                def emit_tail(S, t0, N, nxt):
                    col = S.col
                    isctx = (S is SC)
                    pG, bG, _ = psG.next()
                    stats_rstd(sqb, 0, 8, N, pG, bG, std_t, rstd_t, b_sqb, b_rstd, 1.0 / 1024)
                    for h in range(8):
                        P.op("vector", lambda e, h=h, N=N: e.scalar_tensor_tensor(
                            out=mix[:, h, :N], in0=oo[:, h, :N], scalar=gnas[:, l, h:h + 1], in1=rstd_t[:, :N],
                            op0=ALU.mult, op1=ALU.mult), reads=[b_oo[h], b_rstd, b_const], writes=[b_mix])
                    if dbg and not isctx:
                        P.op("gpsimd", lambda e, S=S, t0=t0, N=N: e.dma_start(
                            out=S.dmix[:, :, t0:t0 + N].rearrange("c p t -> p c t"), in_=mix[:, :, :N]),
                            reads=[b_mix], writes=[Buf()], dsem=s_qt)
                    for f in range(KC):
                        wt_, wbuf, wsem = wb.next()
                        P.op("sync", lambda e, wt_=wt_, u=U_O + f: e.dma_start(out=wt_[:], in_=wtb[l][u]),
                             reads=[b_wtb[l][U_O + f]], writes=[wbuf], dsem=wsem)
                        x_, xb_, xs_ = xc.next()
                        P.op("sync", lambda e, x_=x_, S=S, f=f, t0=t0, N=N: e.dma_start(
                            out=x_[:, :N], in_=S.x[f, :, t0:t0 + N]), reads=dbufs(S, "x", [f], t0, N),
                            writes=[xb_], dsem=xs_)
                        pG, bG, _ = psG.next()

                        def mm(e, wt_=wt_, pG=pG, N=N):
                            ins = None
                            for k in range(KC):
                                ins = e.matmul(pG[:, :N], lhsT=wt_[:, k * 128:(k + 1) * 128], rhs=mix[:, k, :N],
                                               start=(k == 0), stop=(k == KC - 1))
                            return ins
                        P.op("tensor", mm, reads=[wbuf, b_mix], writes=[bG])
                        P.op("vector", lambda e, pG=pG, x_=x_, f=f, N=N, col=col: e.scalar_tensor_tensor(
                            out=oo[:, f, :N], in0=pG[:, :N], scalar=mods[:, l, 32 + f, col:col + 1], in1=x_[:, :N],
                            op0=ALU.mult, op1=ALU.add), reads=[bG, xb_, b_const, b_mods[l]], writes=[b_oo[f]])
                        P.op("gpsimd", lambda e, S=S, f=f, t0=t0, N=N: e.dma_start(
                            out=S.x[f, :, t0:t0 + N], in_=oo[:, f, :N]),
                            reads=[b_oo[f]], writes=dbufs(S, "x", [f], t0, N), dsem=xs_sem[f % 4])
                        P.op("scalar", lambda e, f=f, N=N: e.activation(out=sqb[:, f, :N], in_=oo[:, f, :N], func=AF.Square),
                             reads=[b_oo[f]], writes=[b_sqb])
                        if dbg and not isctx:
                            P.op("gpsimd", lambda e, S=S, f=f, t0=t0, N=N: e.dma_start(
                                out=S.dxn[f, :, t0:t0 + N], in_=oo[:, f, :N]),
                                reads=[b_oo[f]], writes=[Buf()], dsem=xs_sem[f % 4])
                    pG, bG, _ = psG.next()
                    stats_rstd(sqb, 0, KC, N, pG, bG, std_t, rstd_t, b_sqb, b_rstd, 1.0 / D)
                    for c in range(KC):
                        tt, tb, _ = tmp.next()
                        P.op("vector", lambda e, c=c, tt=tt, N=N, col=col: e.scalar_tensor_tensor(
                            out=tt[:, :N], in0=oo[:, c, :N], scalar=A2[:, l, c, col:col + 1], in1=rstd_t[:, :N],
                            op0=ALU.mult, op1=ALU.mult), reads=[b_oo[c], b_rstd, b_const, b_mods[l]], writes=[tb])
                        hs_, hb_, hsem = hst.next()
                        P.op("scalar", lambda e, c=c, tt=tt, hs_=hs_, N=N, col=col: e.activation(
                            out=hs_[:, :N], in_=tt[:, :N], func=AF.Identity, bias=mods[:, l, 48 + c, col:col + 1],
                            scale=1.0), reads=[tb, b_const, b_mods[l]], writes=[hb_])
                        P.op("gpsimd", lambda e, hs_=hs_, S=S, c=c, t0=t0, N=N: e.dma_start(
                            out=S.h2[c, :, t0:t0 + N], in_=hs_[:, :N]),
                            reads=[hb_], writes=dbufs(S, "h2", [c], t0, N), dsem=hsem)

                tlist = [(S, t0, N) for (S, tl) in streams_B for (t0, N) in tl]
                ld = emit_loads(*tlist[0])
                for c in range(8):
                    conv_load(tlist[0][0], tlist[0][1], tlist[0][2], c)
                    conv_chunk(tlist[0][0], tlist[0][1], tlist[0][2], c)
                conv_stats(tlist[0][2])
                for ti_, (S, t0, N) in enumerate(tlist):
                    chunks = ld
                    nxt = tlist[ti_ + 1] if ti_ + 1 < len(tlist) else None
                    emit_heads(S, t0, N, chunks, nxt)
                    if nxt is not None:
                        ld = emit_loads(*nxt)
                    emit_tail(S, t0, N, nxt)
                    pump(7)
                P.end_phase()

            def phaseC(pc):
                def sbc(name, shape, dt=F32):
                    return pc.enter_context(_sb(name, shape, dt))
                h2t = sbc("h2c", [128, KC, 512], BF16)
                b_h2t = Buf()
                s_h2t = P.dsem()
                hid = sbc("hid", [128, 64, 512], BF16)
                b_hid = [Buf() for _ in range(64)]
                rr = Rot(P, pc, nc, "rr", 2, [128, 512], F32, sem=False)
                xc = Rot(P, pc, nc, "xc", 3, [128, 512], F32)
                xs_sem = [P.dsem() for _ in range(3)]
                wb = Rot(P, pc, nc, "wbc", 3, [128, 2048], BF16)
                w2 = Rot(P, pc, nc, "w2c", 2, [128, 8192], BF16)
                pb = Rot(P, pc, nc, "pbc", 4, [128, 512], F32, psum=True)
                bg_steps = []
                bg_pend = []

                def bg_point():
                    if bg_pend:
                        bg_pend.pop(0)()
                    if bg_steps:
                        ld_, cp_ = bg_steps.pop(0)
                        ld_()
                        if cp_ is not None:
                            bg_pend.append(cp_)
                if l + 1 < depth:
                    wa = Rot(P, pc, nc, "wab", 2, [128, KC, 256], BF16)
                    pm = pc.enter_context(_ps("pmb", [128, 96, 2], F32))
                    bg_steps = mods_steps(l + 1, wa, pm, Buf())
                if last:
                    xo = sbc("xo", [128, KC, 512])
                    b_xo = Buf()
                    sqf = sbc("sqf", [128, KC, 512], BF16)
                    b_sqf = Buf()
                    std_t = sbc("stdc", [128, 512])
                    rstd_t = sbc("rstdc", [128, 512])
                    b_rstd = Buf()
                    pst = pc.enter_context(_ps("pstc", [128, 512], F32))
                    b_pst = Buf()
                    ost = Rot(P, pc, nc, "ost", 3, [128, 512], F32)
                def load_h2(S, t0, N):
                    P.op("sync", lambda e: e.dma_start(
                        out=h2t[:, :, :N], in_=S.h2[:, :, t0:t0 + N].rearrange("c p t -> p c t")),
                        reads=dbufs(S, "h2", R16, t0, N), writes=[b_h2t], dsem=s_h2t)
                tlistC = [(S, t0, N) for (S, tl) in streams_B for (t0, N) in tl]
                load_h2(*tlistC[0])
                for tci_, (S, t0, N) in enumerate(tlistC):
                    col = S.col
                    if True:
                        for f in range(64):
                            wt_, wbuf, wsem = wb.next()
                            P.op("sync", lambda e, wt_=wt_, u=U_M1 + f: e.dma_start(out=wt_[:], in_=wtb[l][u]),
                                 reads=[b_wtb[l][U_M1 + f]], writes=[wbuf], dsem=wsem)
                            pt, pbuf, _ = pb.next()

                            def mm(e, wt_=wt_, pt=pt, N=N):
                                ins = None
                                for k in range(KC):
                                    ins = e.matmul(pt[:, :N], lhsT=wt_[:, k * 128:(k + 1) * 128], rhs=h2t[:, k, :N],
                                                   start=(k == 0), stop=(k == KC - 1))
                                return ins
                            P.op("tensor", mm, reads=[wbuf, b_h2t], writes=[pbuf])
                            r_, rb, _ = rr.next()
                            P.op("scalar", lambda e, r_=r_, pt=pt, N=N: e.activation(out=r_[:, :N], in_=pt[:, :N], func=AF.Relu),
                                 reads=[pbuf], writes=[rb])
                            P.op("vector", lambda e, r_=r_, f=f, N=N: e.tensor_tensor(
                                out=hid[:, f, :N], in0=r_[:, :N], in1=r_[:, :N], op=ALU.mult), reads=[rb], writes=[b_hid[f]])
                        for f in range(KC):
                            wt_, wbuf, wsem = w2.next()
                            P.op("sync", lambda e, wt_=wt_, f=f: e.dma_start(
                                out=wt_[:].rearrange("p (u e) -> p u e", u=4),
                                in_=wtb[l][U_M2 + 4 * f:U_M2 + 4 * f + 4].rearrange("u p e -> p u e")),
                                reads=b_wtb[l][U_M2 + 4 * f:U_M2 + 4 * f + 4], writes=[wbuf], dsem=wsem)
                            if f == 1 and tci_ + 1 < len(tlistC):
                                load_h2(*tlistC[tci_ + 1])
                            x_, xb_, xs_ = xc.next()
                            P.op("sync", lambda e, x_=x_, S=S, f=f, t0=t0, N=N: e.dma_start(
                                out=x_[:, :N], in_=S.x[f, :, t0:t0 + N]), reads=dbufs(S, "x", [f], t0, N),
                                writes=[xb_], dsem=xs_)
                            pt, pbuf, _ = pb.next()

                            def mm(e, wt_=wt_, pt=pt, N=N):
                                ins = None
                                for k in range(64):
                                    ins = e.matmul(pt[:, :N], lhsT=wt_[:, k * 128:(k + 1) * 128], rhs=hid[:, k, :N],
                                                   start=(k == 0), stop=(k == 63))
                                return ins
                            P.op("tensor", mm, reads=[wbuf] + b_hid, writes=[pbuf])
                            if not last:
                                P.op("vector", lambda e, x_=x_, pt=pt, f=f, N=N, col=col: e.scalar_tensor_tensor(
                                    out=x_[:, :N], in0=pt[:, :N], scalar=mods[:, l, 80 + f, col:col + 1], in1=x_[:, :N],
                                    op0=ALU.mult, op1=ALU.add), reads=[pbuf, xb_, b_const, b_mods[l]], writes=[xb_])
                                P.op("gpsimd", lambda e, x_=x_, S=S, f=f, t0=t0, N=N: e.dma_start(
                                    out=S.x[f, :, t0:t0 + N], in_=x_[:, :N]), reads=[xb_],
                                    writes=dbufs(S, "x", [f], t0, N), dsem=xs_sem[f % 3])
                            else:
                                P.op("vector", lambda e, x_=x_, pt=pt, f=f, N=N, col=col: e.scalar_tensor_tensor(
                                    out=xo[:, f, :N], in0=pt[:, :N], scalar=mods[:, l, 80 + f, col:col + 1], in1=x_[:, :N],
                                    op0=ALU.mult, op1=ALU.add), reads=[pbuf, xb_, b_const, b_mods[l]], writes=[b_xo])
                        if last:
                            for c4 in range(4):
                                P.op("scalar", lambda e, c4=c4, N=N: e.activation(
                                    out=sqf[:, c4 * 4:(c4 + 1) * 4, :N], in_=xo[:, c4 * 4:(c4 + 1) * 4, :N], func=AF.Square),
                                    reads=[b_xo], writes=[b_sqf])
                            stats_rstd(sqf, 0, KC, N, pst, b_pst, std_t, rstd_t, b_sqf, b_rstd, 1.0 / D)
                            for c in range(KC):
                                o_, ob, os_ = ost.next()
                                P.op("vector", lambda e, c=c, o_=o_, N=N: e.scalar_tensor_tensor(
                                    out=o_[:, :N], in0=xo[:, c, :N], scalar=gfs[:, c:c + 1], in1=rstd_t[:, :N],
                                    op0=ALU.mult, op1=ALU.mult), reads=[b_xo, b_rstd, b_const], writes=[ob])
                                P.op("gpsimd", lambda e, o_=o_, c=c, t0=t0, N=N: e.dma_start(
                                    out=outT[c, :, t0:t0 + N], in_=o_[:, :N]),
                                    reads=[ob], writes=[Buf()], dsem=os_)
                        pump(7)
                        for _ in range(5):
                            if bg_steps:
                                ld_, cp_ = bg_steps.pop(0)
                                ld_()
                                if cp_ is not None:
                                    cp_()
                while bg_steps:
                    ld_, cp_ = bg_steps.pop(0)
                    ld_()
                    if cp_ is not None:
                        cp_()
                pump(len(pending))
                P.end_phase()
            for ph in (phaseA, phaseB, phaseC):
                with ExitStack() as st_:
                    ph(st_)

        for l_ in range(depth):
            layer(l_)
        P.emit()
    return nc


def _fm_units(W):
    K, n = W.shape
    kc = K // 128
    nb = n // 128
    a = W.reshape(kc, 128, nb, 128).transpose(2, 1, 0, 3).reshape(nb, 128, kc * 128)
    if kc == 16:
        return a
    u = kc // 16
    return a.reshape(nb, 128, u, 2048).transpose(0, 2, 1, 3).reshape(nb * u, 128, 2048)


def _v_units(Wv):
    a = Wv.reshape(16, 128, 2, 512).transpose(2, 1, 0, 3).reshape(2, 128, 8192)
    return a.reshape(2, 128, 4, 2048).transpose(0, 2, 1, 3).reshape(8, 128, 2048)


def _build_bias(rpb, rev):
    res = []
    for ti in (0, 2):
        lq = ti * 512 + np.arange(512)
        lk = ti * 512 - 256 + np.arange(1024)
        gq = (SEQ - 1 - lq) if rev else lq
        gk = (SEQ - 1 - lk) if rev else lk
        rq, cq = gq // 64, gq % 64
        rk, ck = gk // 64, gk % 64
        rs = np.clip(rq - 4, 0, 120)
        cs = np.clip(cq - 8, 0, 48)
        valid = ((lk >= 0)[:, None] & (rk[:, None] >= rs[None]) & (rk[:, None] < rs[None] + 8)
                 & (ck[:, None] >= cs[None]) & (ck[:, None] < cs[None] + 16))
        dr = np.clip(rk[:, None] - rq[None] + 7, 0, 14)
        dc = np.clip(ck[:, None] - cq[None] + 15, 0, 30)
        vals = rpb[:, :, dr, dc]
        res.append(np.where(valid[None, None], vals, np.float32(NEG)).astype(np.float32))
    edge = res[0].reshape(DEPTH, 8, 8, 128, 512).transpose(0, 1, 3, 2, 4).reshape(DEPTH, 8, 128, 4096)
    blk = np.full((DEPTH, 8, 128, 11, 64), np.float32(NEG), np.float32)
    for d0 in range(-1, 9):
        a = d0 % 2
        j = (d0 + a) // 2
        blk[:, :, :, d0 + 1, :] = res[1][:, :, j * 128:(j + 1) * 128, a * 64:(a + 1) * 64]
    return np.ascontiguousarray(edge), np.ascontiguousarray(blk.reshape(DEPTH, 8, 128, 704)), res[1]


def _fm(v, nchunk):
    sh = v.shape[:-1]
    a = v.reshape(sh + (nchunk, 128))
    return np.ascontiguousarray(np.moveaxis(a, -1, 0))


_NC_CACHE = {}


def _prep(x, c, ctx, c_ctx, w_ada, b_ada, g_norm1, w_in, rpb, conv_w, g_na_out, g_conv_out,
          w_out, g_norm2, w_mlp1, w_mlp2, g_final):
    f = lambda a: np.asarray(a, dtype=np.float32)
    x, c, ctx, c_ctx, w_ada, b_ada, g_norm1, w_in, rpb, conv_w = map(f, (x, c, ctx, c_ctx, w_ada, b_ada, g_norm1, w_in, rpb, conv_w))
    g_na_out, g_conv_out, w_out, g_norm2, w_mlp1, w_mlp2, g_final = map(f, (g_na_out, g_conv_out, w_out, g_norm2, w_mlp1, w_mlp2, g_final))
    wts = np.empty((DEPTH, NU, 128, 2048), np.float32)
    for l in range(DEPTH):
        wi = w_in[l]
        wts[l, U_Q:U_Q + 8] = _fm_units(wi[:, 0:1024])
        wts[l, U_K:U_K + 8] = _fm_units(wi[:, 1024:2048])
        wts[l, U_V:U_V + 8] = _v_units(wi[:, 2048:3072])
        wts[l, U_BG:U_BG + 8] = _fm_units(wi[:, 3072:4096])
        wts[l, U_CG:U_CG + 8] = _fm_units(wi[:, 4096:5120])
        wts[l, U_U:U_U + 8] = _fm_units(wi[:, 5120:6144])
        wts[l, U_O:U_O + 16] = _fm_units(w_out[l])
        wts[l, U_M1:U_M1 + 64] = _fm_units(w_mlp1[l])
        wts[l, U_M2:U_M2 + 64] = _fm_units(w_mlp2[l])
    shared = {
        "w_ada": np.ascontiguousarray(w_ada),
        "b_ada": _fm(b_ada, 96),
        "g1": _fm(g_norm1, KC), "g2": _fm(g_norm2, KC), "gf": _fm(g_final, KC),
        "gna": _fm(g_na_out, 8), "gcv": _fm(g_conv_out, 8),
        "wts": wts,
        "ident": np.eye(128, dtype=np.float32),
    }
    cwf = _fm(conv_w, 8)
    cw_fwd = np.ascontiguousarray(cwf.transpose(0, 1, 3, 2))
    cw_rev = np.ascontiguousarray(cw_fwd[:, :, :, ::-1])
    bias_fwd, blk_fwd, _ = _build_bias(rpb, False)
    bias_rev, blk_rev, _ = _build_bias(rpb, True)
    ccx = _fm(c_ctx, KC)
    in_maps = []
    gidxs = []
    for core in range(8):
        b, rev = core // 2, core % 2
        gidx = (SEQ - 1 - np.arange(NT)) if rev else np.arange(NT)
        gidxs.append(gidx)
        xT = np.ascontiguousarray(x[b][gidx].T).reshape(KC, 128, NT)
        cb = ctx[b][::-1] if rev else ctx[b]
        cT = np.ascontiguousarray(cb.T).reshape(KC, 128, CTXL)
        cc = np.ascontiguousarray(np.stack([_fm(c[b], KC), ccx], axis=-1))
        m = dict(shared)
        m.update({"xT": xT, "ctxT": cT, "cc": cc, "cw": cw_rev if rev else cw_fwd,
                  "bias": bias_rev if rev else bias_fwd, "blk": blk_rev if rev else blk_fwd})
        in_maps.append(m)
    return in_maps, gidxs


def kernel(x, c, ctx, c_ctx, w_ada, b_ada, g_norm1, w_in, rpb, conv_w, g_na_out, g_conv_out,
           w_out, g_norm2, w_mlp1, w_mlp2, g_final):
    in_maps, gidxs = _prep(x, c, ctx, c_ctx, w_ada, b_ada, g_norm1, w_in, rpb, conv_w, g_na_out, g_conv_out,
                           w_out, g_norm2, w_mlp1, w_mlp2, g_final)
    if "nc" not in _NC_CACHE:
        _NC_CACHE["nc"] = build()
    res = run_bass_kernel_spmd(_NC_CACHE["nc"], in_maps, core_ids=list(range(8)))
    out = np.empty((4, SEQ, D), np.float32)
    for core in range(8):
        b = core // 2
        o = np.asarray(res.results[core]["outT"]).reshape(D, OWN)
        out[b, gidxs[core][:OWN], :] = o.T
    return out
```

```python
import numpy as np
from contextlib import ExitStack
import concourse.bass as bass
import concourse.mybir as mybir
from concourse.bass_utils import run_bass_kernel_spmd

F32 = mybir.dt.float32
BF16 = mybir.dt.bfloat16
AF = mybir.ActivationFunctionType
ALU = mybir.AluOpType
ENGS = ["sync", "gpsimd", "scalar", "vector", "tensor"]

D = 2048
KC = 16
SEQ = 8192
OWN = 4096
NT = 5120
CTXL = 256
DEPTH = 4
NU = 192
U_Q, U_K, U_BG, U_CG, U_U, U_V, U_O, U_M1, U_M2 = 0, 8, 16, 24, 32, 40, 48, 64, 128
NEG = -30000.0
EPS = 1e-6


class Buf:
    def __init__(self, name=""):
        self.name = name
        self.w = None
        self.r = []


class DSem:
    def __init__(self, h, key):
        self.h = h
        self.key = key
        self.count = 0
        self.last = None
        self.nobarrier = False


class Prog:
    def __init__(self, nc, es):
        self.nc = nc
        self.es = es
        self.ops = {e: [] for e in ENGS}
        self.cnt = {e: 0 for e in ENGS}
        self.waited = {e: {} for e in ENGS}
        self.psem = {e: es.enter_context(nc.semaphore("p_" + e)) for e in ENGS}
        self.dsems = []
        self.free = []
        self.phase = []

    def dsem(self, persistent=False):
        if not persistent and self.free:
            s = self.free.pop()
        else:
            n = len(self.dsems)
            s = DSem(self.es.enter_context(self.nc.semaphore("d%d" % n)), "d%d" % n)
            self.dsems.append(s)
        if not persistent:
            self.phase.append(s)
        return s

    def end_phase(self):
        self.barrier()
        self.free.extend(self.phase)
        self.phase = []

    def op(self, eng, fn, reads=(), writes=(), dsem=None):
        waits = []
        for b in reads:
            if b.w is not None:
                waits.append(b.w)
        for b in writes:
            if b.w is not None:
                waits.append(b.w)
            waits.extend(b.r)
        if dsem is not None:
            if dsem.last is not None:
                waits.append(dsem.last)
            dsem.count += 16
            t = (dsem.h, dsem.count, dsem.key)
            dsem.last = t
            inc = (dsem.h, 16)
        else:
            self.cnt[eng] += 1
            t = (self.psem[eng], self.cnt[eng], eng)
            inc = (self.psem[eng], 1)
        self._emit(eng, waits, fn, inc)
        for b in reads:
            b.r.append(t)
            if len(b.r) > 64:
                b.r = b.r[-64:] if False else b.r
        for b in writes:
            b.w = t
            b.r = []
        return t

    def _emit(self, eng, waits, fn, inc):
        best = {}
        for (h, v, key) in waits:
            if key == eng and eng == "tensor":
                continue
            if v <= self.waited[eng].get(key, 0):
                continue
            if key not in best or best[key][1] < v:
                best[key] = (h, v)
        ws = []
        for key, (h, v) in best.items():
            self.waited[eng][key] = v
            ws.append((h, v))

        def run(e, ws=ws, fn=fn, inc=inc):
            for (h, v) in ws:
                e.wait_ge(h, v)
            if fn is not None:
                ins = fn(e)
                if inc is not None:
                    ins.then_inc(inc[0], inc[1])
        self.ops[eng].append(run)

    def barrier(self):
        ts = [(self.psem[e], self.cnt[e], e) for e in ENGS if self.cnt[e] > 0]
        ts += [s.last for s in self.dsems if s.last is not None and not s.nobarrier]
        for e in ENGS:
            self._emit(e, [t for t in ts if t[2] != e], None, None)

    def emit(self):
        with self.nc.Block() as block:
            @block.sync
            def _(e):
                for f in self.ops["sync"]:
                    f(e)

            @block.gpsimd
            def _(e):
                for f in self.ops["gpsimd"]:
                    f(e)

            @block.scalar
            def _(e):
                for f in self.ops["scalar"]:
                    f(e)

            @block.vector
            def _(e):
                for f in self.ops["vector"]:
                    f(e)

            @block.tensor
            def _(e):
                for f in self.ops["tensor"]:
                    f(e)


_UNIQ = [0]


class Rot:
    def __init__(self, P, es, nc, name, n, shape, dt, psum=False, sem=True):
        mk = nc.psum_tensor if psum else nc.sbuf_tensor
        _UNIQ[0] += 1
        self.t = [es.enter_context(mk("%s%d_%d" % (name, i, _UNIQ[0]), shape, dt)) for i in range(n)]
        self.b = [Buf("%s%d" % (name, i)) for i in range(n)]
        self.s = [P.dsem() for _ in range(n)] if (sem and not psum) else [None] * n
        self.i = 0
        self.n = n

    def next(self):
        i = self.i
        self.i = (self.i + 1) % self.n
        return self.t[i], self.b[i], self.s[i]


def n_in(l):
    return OWN + 256 * (4 - l)


def n_out(l):
    return OWN + 256 * (3 - l)


def tiles_of(n):
    out = []
    t = 0
    while t < n:
        w = min(512, n - t)
        out.append((t, w))
        t += w
    return out


def build(depth=DEPTH, dbg=False):
    NL = depth if dbg else DEPTH
    nc = bass.Bass("TRN2", target_bir_lowering=False)
    _osb, _ops = nc.sbuf_tensor, nc.psum_tensor

    def _sb(name, *a, **k):
        _UNIQ[0] += 1
        return _osb("%s_%d" % (name, _UNIQ[0]), *a, **k)

    def _ps(name, *a, **k):
        _UNIQ[0] += 1
        return _ops("%s_%d" % (name, _UNIQ[0]), *a, **k)

    def din(name, shape, dt=F32):
        return nc.dram_tensor(name, shape, dt, kind="ExternalInput").ap()

    def dscr(name, shape, dt):
        return nc.dram_tensor(name, shape, dt).ap()

    xin = din("xT", [KC, 128, NT])
    cin = din("ctxT", [KC, 128, CTXL])
    cc = din("cc", [128, KC, 2])
    w_ada = din("w_ada", [NL, D, 6 * D])
    b_ada = din("b_ada", [128, DEPTH, 96])
    g1 = din("g1", [128, DEPTH, KC])
    g2 = din("g2", [128, DEPTH, KC])
    gf = din("gf", [128, KC])
    gna = din("gna", [128, DEPTH, 8])
    gcv = din("gcv", [128, DEPTH, 8])
    cw = din("cw", [128, DEPTH, 8, 3])
    idin = din("ident", [128, 128])
    wts = din("wts", [NL, NU, 128, 2048])
    bias = din("bias", [NL, 8, 128, 4096])
    blk = din("blk", [NL, 8, 128, 704])
    outT = nc.dram_tensor("outT", [KC, 128, OWN], F32, kind="ExternalOutput").ap()

    wtb = [dscr("wtb%d" % l_, [NU, 128, 2048], BF16) for l_ in range(DEPTH)]
    biasb = [dscr("biasb%d" % l_, [8, 128, 4096], BF16) for l_ in range(DEPTH)]
    blkb = [dscr("blkb%d" % l_, [8, 128, 704], BF16) for l_ in range(DEPTH)]

    class Stream:
        pass

    def mkstream(name, n, col):
        s = Stream()
        s.name = name
        s.col = col
        s.n = n
        if dbg and name == "m":
            s.x = nc.dram_tensor("dbg_x", [KC, 128, n], F32, kind="ExternalOutput").ap()
        else:
            s.x = dscr(name + "_x", [KC, 128, n], F32)
        mk = dscr
        if dbg and name == "m":
            mk = lambda nm, sh, dt: nc.dram_tensor("dbg_" + nm, sh, dt, kind="ExternalOutput").ap()
        s.q = mk(name + "_q", [8, 128, n], BF16)
        s.k = mk(name + "_k", [8, 128, n], BF16)
        s.v = mk(name + "_v", [n, 1024], BF16)
        s.z = mk(name + "_z", [8, 128, n + 2], F32)
        s.bg = mk(name + "_bg", [8, 128, n], F32)
        s.h2 = mk(name + "_h2", [KC, 128, n], BF16)
        if dbg and name == "m":
            s.dmix = mk("mix", [KC, 128, n], BF16)
            s.dxn = mk("xn", [KC, 128, n], F32)
        s.b = {}
        return s

    def dbufs(s, what, cs, t0, n):
        out = []
        for c in cs:
            for t in range((t0 // 128) * 128, t0 + n, 128):
                key = (what, c, t // 128)
                if key not in s.b:
                    s.b[key] = Buf("%s_%s_%d_%d" % (s.name, what, c, t))
                out.append(s.b[key])
        return out

    R8 = list(range(8))
    R16 = list(range(16))
    SM = mkstream("m", NT, 0)
    SC = mkstream("c", CTXL, 1)
    b_wtb = [[Buf() for _ in range(NU)] for _ in range(DEPTH)]
    b_biasb = [[Buf() for _ in range(8)] for _ in range(DEPTH)]
    b_blkb = [[Buf() for _ in range(8)] for _ in range(DEPTH)]

    with ExitStack() as es:
        P = Prog(nc, es)
        def sb(name, shape, dt=F32):
            return es.enter_context(_sb(name, shape, dt))
        mods = sb("mods", [128, DEPTH, 96, 2])
        A1 = sb("A1", [128, DEPTH, KC, 2])
        A2 = sb("A2", [128, DEPTH, KC, 2])
        g1s = sb("g1s", [128, DEPTH, KC])
        g2s = sb("g2s", [128, DEPTH, KC])
        gfs = sb("gfs", [128, KC])
        gnas = sb("gnas", [128, DEPTH, 8])
        gcvs = sb("gcvs", [128, DEPTH, 8])
        cws = sb("cws", [128, DEPTH, 8, 3])
        bas = sb("bas", [128, DEPTH, 96])
        ones = sb("ones", [128, 128], BF16)
        ident = sb("ident_b", [128, 128], BF16)
        identf = sb("identf", [128, 128])
        epst = sb("epst", [128, 1])
        zpad = sb("zpad", [128, 8, 1])
        b_const = Buf("const")
        s_const = P.dsem(persistent=True)
        s_cast = [P.dsem(persistent=True) for _ in range(8)]
        for s_ in s_cast:
            s_.nobarrier = True
        castn = [0]

        def cast_dma(out_ap, in_ap, wbuf):
            s = s_cast[castn[0] % 8]
            castn[0] += 1
            P.op("gpsimd", lambda e: e.dma_start(out=out_ap, in_=in_ap, max_dma_last_dim=8192),
                 writes=[wbuf], dsem=s)

        def cast_jobs(l):
            jobs = []
            order = (list(range(0, 48)) + list(range(48, 192)))
            for u in order:
                jobs.append((wtb[l][u], wts[l, u], b_wtb[l][u]))
                if u == 47:
                    for h in range(8):
                        jobs.append((biasb[l][h], bias[l, h], b_biasb[l][h]))
                        jobs.append((blkb[l][h], blk[l, h], b_blkb[l][h]))
            return jobs

        pending = []

        def pump(n):
            for _ in range(n):
                if pending:
                    cast_dma(*pending.pop(0))

        for (dst, src) in [(g1s, g1), (g2s, g2), (gfs, gf), (gnas, gna), (gcvs, gcv), (cws, cw), (bas, b_ada),
                           (identf, idin)]:
            P.op("sync", lambda e, dst=dst, src=src: e.dma_start(out=dst[:], in_=src), writes=[b_const], dsem=s_const)
        P.op("vector", lambda e: e.memset(ones[:], 1.0), writes=[b_const])
        P.op("vector", lambda e: e.memset(epst[:], EPS), writes=[b_const])
        P.op("vector", lambda e: e.memset(zpad[:], 0.0), writes=[b_const])
        P.op("vector", lambda e: e.tensor_copy(out=ident[:], in_=identf[:]), reads=[b_const], writes=[b_const])
        for s in (SM, SC):
            for col in (0, s.n + 1):
                P.op("gpsimd", lambda e, s=s, col=col: e.dma_start(
                    out=s.z[:, :, col:col + 1].rearrange("c p t -> p c t"), in_=zpad[:], allow_slow_non_contiguous=True),
                    reads=[b_const], writes=[Buf()], dsem=s_const)
        s_init = [P.dsem() for _ in range(4)]
        for (s, src) in ((SM, xin), (SC, cin)):
            for c in range(KC):
                P.op("gpsimd", lambda e, s=s, src=src, c=c: e.dma_start(out=s.x[c], in_=src[c]),
                     writes=[Buf()], dsem=s_init[c % 4])
        pending.extend(cast_jobs(0))
        pump(len(pending))

        sl = sb("sl", [128, KC, 2])
        sls = sb("sls", [128, KC, 2])
        slsb = sb("slsb", [128, KC, 2], BF16)
        b_sl = Buf()
        b_mods = [Buf() for _ in range(DEPTH)]

        def mods_steps(l, wa, pm, b_pm):
            steps = []
            for fg in range(48):
                def mk(fg=fg):
                    st = {}

                    def load():
                        t, b, s = wa.next()
                        st["t"], st["b"] = t, b
                        P.op("gpsimd", lambda e, t=t: e.dma_start(
                            out=t[:], in_=w_ada[l, :, fg * 256:(fg + 1) * 256].rearrange("(k p) n -> p k n", p=128)),
                            writes=[b], dsem=s)

                    def comp():
                        t, b = st["t"], st["b"]

                        def mm(e, t=t):
                            ins = None
                            for f2 in range(2):
                                for k in range(KC):
                                    ins = e.matmul(pm[:, fg * 2 + f2, :], lhsT=t[:, k, f2 * 128:(f2 + 1) * 128], rhs=slsb[:, k, :],
                                                   start=(k == 0), stop=(k == KC - 1))
                            return ins
                        P.op("tensor", mm, reads=[b, b_sl], writes=[b_pm])
                    return load, comp
                steps.append(mk())

            def fin():
                for col in range(2):
                    P.op("vector", lambda e, col=col: e.tensor_tensor(
                        out=mods[:, l, :, col], in0=pm[:, :, col], in1=bas[:, l, :], op=ALU.add),
                        reads=[b_pm, b_const], writes=[b_mods[l]])
                    for (A, g, off) in ((A1, g1s, 16), (A2, g2s, 64)):
                        P.op("vector", lambda e, col=col, A=A, g=g, off=off: e.scalar_tensor_tensor(
                            out=A[:, l, :, col], in0=mods[:, l, off:off + 16, col], scalar=1.0, in1=g[:, l, :],
                            op0=ALU.add, op1=ALU.mult), reads=[b_mods[l], b_const], writes=[b_mods[l]])
            steps.append((fin, None))
            return steps

        with ExitStack() as ps_:
            s_sl = P.dsem()
            wa = Rot(P, ps_, nc, "wa", 2, [128, KC, 256], BF16)
            pm = ps_.enter_context(_ps("pm", [128, 96, 2], F32))
            b_pm = Buf()
            P.op("sync", lambda e: e.dma_start(out=sl[:], in_=cc), writes=[b_sl], dsem=s_sl)
            P.op("scalar", lambda e: e.activation(out=sls[:], in_=sl[:], func=AF.Silu), reads=[b_sl], writes=[b_sl])
            P.op("vector", lambda e: e.tensor_copy(out=slsb[:], in_=sls[:]), reads=[b_sl], writes=[b_sl])
            for (ld_, cp_) in mods_steps(0, wa, pm, b_pm):
                ld_()
                if cp_ is not None:
                    cp_()
            if dbg:
                dmods = nc.dram_tensor("dbg_mods", [128, DEPTH * 96 * 2], F32, kind="ExternalOutput").ap()
                P.op("gpsimd", lambda e: e.dma_start(out=dmods, in_=mods[:].rearrange("p a b c -> p (a b c)")),
                     reads=[b_mods[0]], writes=[Buf()], dsem=s_sl)
                dbias = nc.dram_tensor("dbg_bias", [128, 4096], BF16, kind="ExternalOutput").ap()
                P.op("gpsimd", lambda e: e.dma_start(out=dbias, in_=biasb[0][3]),
                     reads=[b_biasb[0][3]], writes=[Buf()], dsem=s_sl)
            P.end_phase()

        def stats_rstd(sq_t, c0, nchunks, N, pbank, b_pbank, std_t, rstd_t, b_sq, b_rstd, inv_n):
            def mm(e):
                ins = None
                for c in range(nchunks):
                    ins = e.matmul(pbank[:, :N], lhsT=ones[:], rhs=sq_t[:, c0 + c, :N], start=(c == 0),
                                   stop=(c == nchunks - 1))
                return ins
            P.op("tensor", mm, reads=[b_sq, b_const], writes=[b_pbank])
            P.op("scalar", lambda e: e.activation(out=std_t[:, :N], in_=pbank[:, :N], func=AF.Sqrt, bias=epst[:, 0:1],
                                                  scale=inv_n), reads=[b_pbank, b_const], writes=[b_rstd])
            P.op("vector", lambda e: e.reciprocal(out=rstd_t[:, :N], in_=std_t[:, :N]), reads=[b_rstd], writes=[b_rstd])

        def layer(l):
            last = (l == DEPTH - 1)
            if l + 1 < depth:
                pending.extend(cast_jobs(l + 1))
            streams_A = [(SC, [(0, CTXL)]), (SM, tiles_of(n_in(l)))]
            streams_B = ([] if last else [(SC, [(0, CTXL)])]) + [(SM, tiles_of(n_out(l)))]
            def phaseA(pa):
                xts = [pa.enter_context(_sb("xt%d" % i_, [128, KC, 512], F32)) for i_ in range(2)]
                b_xts = [Buf(), Buf()]
                s_xts = [P.dsem(), P.dsem()]
                sq = pa.enter_context(_sb("sq", [128, KC, 512], BF16))
                b_sq = Buf()
                hts = [pa.enter_context(_sb("ht%d" % i_, [128, KC, 512], BF16)) for i_ in range(2)]
                b_hts = [Buf(), Buf()]
                cgt = pa.enter_context(_sb("cgt", [128, 8, 512], F32))
                b_cg = [Buf() for _ in range(8)]
                std_t = pa.enter_context(_sb("std", [128, 512], F32))
                rstd_t = pa.enter_context(_sb("rstd", [128, 512], F32))
                b_rstd = Buf()
                tmp = Rot(P, pa, nc, "tmp", 2, [128, 512], F32, sem=False)
                stb = Rot(P, pa, nc, "stb", 3, [128, 512], BF16)
                stf = Rot(P, pa, nc, "stf", 3, [128, 512], F32)
                wb = Rot(P, pa, nc, "wb", 3, [128, 2048], BF16)
                wv = Rot(P, pa, nc, "wv", 2, [128, 8192], BF16)
                pst = pa.enter_context(_ps("pst", [128, 512], F32))
                b_pst = Buf()
                pb = Rot(P, pa, nc, "pb", 4, [128, 512], F32, psum=True)
                def norm_ops(i, S, t0, N):
                    xt, b_xt, s_xt = xts[i % 2], b_xts[i % 2], s_xts[i % 2]
                    ht, b_ht = hts[i % 2], b_hts[i % 2]
                    col = S.col
                    ops = []

                    def load():
                        P.op("sync", lambda e: e.dma_start(
                            out=xt[:, :, :N], in_=S.x[:, :, t0:t0 + N].rearrange("c p t -> p c t")),
                            reads=dbufs(S, "x", R16, t0, N), writes=[b_xt], dsem=s_xt)
                    ops.append(load)
                    ops.extend([None, None, None])

                    def stats():
                        for c4 in range(4):
                            P.op("scalar", lambda e, c4=c4: e.activation(
                                out=sq[:, c4 * 4:(c4 + 1) * 4, :N], in_=xt[:, c4 * 4:(c4 + 1) * 4, :N], func=AF.Square),
                                reads=[b_xt], writes=[b_sq])
                        stats_rstd(sq, 0, KC, N, pst, b_pst, std_t, rstd_t, b_sq, b_rstd, 1.0 / D)
                    ops.append(stats)
                    for c in range(KC):
                        def chunk(c=c):
                            tt, tb, _ = tmp.next()
                            P.op("vector", lambda e: e.scalar_tensor_tensor(
                                out=tt[:, :N], in0=xt[:, c, :N], scalar=A1[:, l, c, col:col + 1], in1=rstd_t[:, :N],
                                op0=ALU.mult, op1=ALU.mult), reads=[b_xt, b_rstd, b_const, b_mods[l]], writes=[tb])
                            P.op("scalar", lambda e: e.activation(
                                out=ht[:, c, :N], in_=tt[:, :N], func=AF.Identity, bias=mods[:, l, c, col:col + 1], scale=1.0),
                                reads=[tb, b_const, b_mods[l]], writes=[b_ht])
                        ops.append(chunk)
                    return ops

                def inproj(S, t0, N, ht, b_ht, hook):
                    for (u0, kind) in ((U_Q, "q"), (U_K, "k"), (U_BG, "bg"), (U_CG, "cg"), (U_U, "u")):
                        for c in range(8):
                            wt_, wbuf, wsem = wb.next()
                            P.op("sync", lambda e, wt_=wt_, u=u0 + c: e.dma_start(out=wt_[:], in_=wtb[l][u]),
                                 reads=[b_wtb[l][u0 + c]], writes=[wbuf], dsem=wsem)
                            pt, pbuf, _ = pb.next()

                            def mm(e, wt_=wt_, pt=pt, N=N):
                                ins = None
                                for k in range(KC):
                                    ins = e.matmul(pt[:, :N], lhsT=wt_[:, k * 128:(k + 1) * 128], rhs=ht[:, k, :N],
                                                   start=(k == 0), stop=(k == KC - 1))
                                return ins
                            P.op("tensor", mm, reads=[wbuf, b_ht], writes=[pbuf])
                            hook()
                            if kind == "q":
                                st, sb_, ss = stb.next()
                                P.op("scalar", lambda e, st=st, pt=pt, N=N: e.activation(
                                    out=st[:, :N], in_=pt[:, :N], func=AF.Identity, scale=float(128 ** -0.5)),
                                    reads=[pbuf], writes=[sb_])
                                P.op("gpsimd", lambda e, st=st, S=S, c=c, t0=t0, N=N: e.dma_start(
                                    out=S.q[c, :, t0:t0 + N], in_=st[:, :N]), reads=[sb_],
                                    writes=dbufs(S, "q", [c], t0, N), dsem=ss)
                            elif kind == "k":
                                st, sb_, ss = stb.next()
                                P.op("vector", lambda e, st=st, pt=pt, N=N: e.tensor_copy(out=st[:, :N], in_=pt[:, :N]),
                                     reads=[pbuf], writes=[sb_])
                                P.op("gpsimd", lambda e, st=st, S=S, c=c, t0=t0, N=N: e.dma_start(
                                    out=S.k[c, :, t0:t0 + N], in_=st[:, :N]), reads=[sb_],
                                    writes=dbufs(S, "k", [c], t0, N), dsem=ss)
                            elif kind == "bg":
                                st, sb_, ss = stf.next()
                                P.op("scalar", lambda e, st=st, pt=pt, N=N: e.activation(
                                    out=st[:, :N], in_=pt[:, :N], func=AF.Copy), reads=[pbuf], writes=[sb_])
                                P.op("gpsimd", lambda e, st=st, S=S, c=c, t0=t0, N=N: e.dma_start(
                                    out=S.bg[c, :, t0:t0 + N], in_=st[:, :N]), reads=[sb_],
                                    writes=dbufs(S, "bg", [c], t0, N), dsem=ss)
                            elif kind == "cg":
                                P.op("vector", lambda e, c=c, pt=pt, N=N: e.tensor_copy(out=cgt[:, c, :N], in_=pt[:, :N]),
                                     reads=[pbuf], writes=[b_cg[c]])
                            else:
                                st, sb_, ss = stf.next()
                                P.op("vector", lambda e, st=st, pt=pt, c=c, N=N: e.tensor_tensor(
                                    out=st[:, :N], in0=pt[:, :N], in1=cgt[:, c, :N], op=ALU.mult),
                                    reads=[pbuf, b_cg[c]], writes=[sb_])
                                P.op("gpsimd", lambda e, st=st, S=S, c=c, t0=t0, N=N: e.dma_start(
                                    out=S.z[c, :, 1 + t0:1 + t0 + N], in_=st[:, :N]), reads=[sb_],
                                    writes=dbufs(S, "z", [c], t0, N), dsem=ss)
                    for g in range(2):
                        wt_, wbuf, wsem = wv.next()
                        P.op("sync", lambda e, wt_=wt_, g=g: e.dma_start(
                            out=wt_[:].rearrange("p (u e) -> p u e", u=4),
                            in_=wtb[l][U_V + 4 * g:U_V + 4 * g + 4].rearrange("u p e -> p u e")),
                            reads=b_wtb[l][U_V + 4 * g:U_V + 4 * g + 4], writes=[wbuf], dsem=wsem)
                        for tb_ in range(N // 128):
                            pt, pbuf, _ = pb.next()

                            def mm(e, wt_=wt_, pt=pt, tb_=tb_):
                                ins = None
                                for k in range(KC):
                                    ins = e.matmul(pt[:, :], lhsT=ht[:, k, tb_ * 128:(tb_ + 1) * 128],
                                                   rhs=wt_[:, k * 512:(k + 1) * 512], start=(k == 0), stop=(k == KC - 1))
                                return ins
                            P.op("tensor", mm, reads=[wbuf, b_ht], writes=[pbuf])
                            hook()
                            st, sb_, ss = stb.next()
                            if tb_ % 2 == 0:
                                P.op("scalar", lambda e, st=st, pt=pt: e.activation(out=st[:], in_=pt[:], func=AF.Copy),
                                     reads=[pbuf], writes=[sb_])
                            else:
                                P.op("vector", lambda e, st=st, pt=pt: e.tensor_copy(out=st[:], in_=pt[:]),
                                     reads=[pbuf], writes=[sb_])
                            P.op("gpsimd", lambda e, st=st, S=S, t0=t0, tb_=tb_, g=g: e.dma_start(
                                out=S.v[t0 + tb_ * 128:t0 + (tb_ + 1) * 128, g * 512:(g + 1) * 512], in_=st[:]),
                                reads=[sb_], writes=dbufs(S, "v", [g], t0 + tb_ * 128, 128), dsem=ss)

                tlistA = [(S, t0, N) for (S, tl) in streams_A for (t0, N) in tl]
                for op_ in norm_ops(0, *tlistA[0]):
                    if op_ is not None:
                        op_()
                for i_, (S, t0, N) in enumerate(tlistA):
                    bgq = norm_ops(i_ + 1, *tlistA[i_ + 1]) if i_ + 1 < len(tlistA) else []
                    cnt = [0]

                    def hook(bgq=bgq, cnt=cnt):
                        cnt[0] += 1
                        if bgq and cnt[0] % 2 == 0:
                            op_ = bgq.pop(0)
                            if op_ is not None:
                                op_()
                    inproj(S, t0, N, hts[i_ % 2], b_hts[i_ % 2], hook)
                    while bgq:
                        op_ = bgq.pop(0)
                        if op_ is not None:
                            op_()
                    pump(7)
                P.end_phase()

            def phaseB(pbk):
                def sbp(name, shape, dt=F32):
                    return pbk.enter_context(_sb(name, shape, dt))
                kctx = sbp("kctx", [128, 8, CTXL], BF16)
                vctx = sbp("vctx", [128, 2, 1024], BF16)
                b_ctxkv = Buf()
                s_ctxkv = [P.dsem(), P.dsem()]
                qt = sbp("qt", [128, 8, 512], BF16)
                b_qt = Buf()
                s_qt = P.dsem()
                kt = sbp("kt", [128, 8, 1024], BF16)
                b_kt = Buf()
                s_kt = P.dsem()
                vt = sbp("vt", [128, 8, 1024], BF16)
                b_vt = Buf()
                s_vt = P.dsem()
                bs = Rot(P, pbk, nc, "bs", 1, [128, 8, 512], BF16)
                blkt = sbp("blkt", [128, 8, 704], BF16)
                b_blk = Buf()
                s_blk = P.dsem()
                P.op("sync", lambda e: e.dma_start(out=blkt[:], in_=blkb[l].rearrange("h p e -> p h e")),
                     reads=b_blkb[l], writes=[b_blk], dsem=s_blk)
                ptl = Rot(P, pbk, nc, "ptl", 3, [128, 512], BF16, sem=False)
                oo = sbp("oo", [128, KC, 512], F32)
                b_oo = [Buf() for _ in range(KC)]
                sqb = sbp("sqb", [128, KC, 512], BF16)
                b_sqb = Buf()
                rden = Rot(P, pbk, nc, "rden", 2, [128, 512], F32, sem=False)
                zt = Rot(P, pbk, nc, "zt", 2, [128, 514], F32)
                bgt = Rot(P, pbk, nc, "bgt", 2, [128, 512], F32)
                acc = Rot(P, pbk, nc, "acc", 2, [128, 512], F32, sem=False)
                mix = sbp("mix", [128, KC, 512], BF16)
                b_mix = Buf()
                ocv = sbp("ocv", [128, 8, 512], BF16)
                b_ocv = [Buf() for _ in range(8)]
                sqc = sbp("sqc", [128, 8, 512], BF16)
                b_sqc = Buf()
                stdc_t = sbp("stdcv", [128, 512])
                rstdc_t = sbp("rstdcv", [128, 512])
                b_rstdc = Buf()
                xc = Rot(P, pbk, nc, "xcb", 2, [128, 512], F32)
                xs_sem = [P.dsem() for _ in range(4)]
                hst = Rot(P, pbk, nc, "hst", 2, [128, 512], BF16)
                std_t = sbp("stdb", [128, 512])
                rstd_t = sbp("rstdb", [128, 512])
                b_rstd = Buf()
                tmp = Rot(P, pbk, nc, "tmpb", 2, [128, 512], F32, sem=False)
                wb = Rot(P, pbk, nc, "wbb", 3, [128, 2048], BF16)
                psS = Rot(P, pbk, nc, "psS", 2, [128, 512], F32, psum=True)
                psO = Rot(P, pbk, nc, "psO", 2, [128, 512], F32, psum=True)
                psD = Rot(P, pbk, nc, "psD", 2, [128, 512], F32, psum=True)
                psG = Rot(P, pbk, nc, "psG", 2, [128, 512], F32, psum=True)
                P.op("sync", lambda e: e.dma_start(out=kctx[:], in_=SC.k.rearrange("h p t -> p h t")),
                     reads=dbufs(SC, "k", R8, 0, CTXL), writes=[b_ctxkv], dsem=s_ctxkv[0])
                P.op("sync", lambda e: e.dma_start(out=vctx[:], in_=SC.v.rearrange("(c p) f -> p c f", p=128)),
                     reads=dbufs(SC, "v", [0, 1], 0, CTXL), writes=[b_ctxkv], dsem=s_ctxkv[1])
                def emit_loads(S, t0, N):
                    isctx = (S is SC)
                    P.op("sync", lambda e, S=S, t0=t0, N=N: e.dma_start(
                        out=qt[:, :, :N], in_=S.q[:, :, t0:t0 + N].rearrange("h p t -> p h t")),
                        reads=dbufs(S, "q", R8, t0, N), writes=[b_qt], dsem=s_qt)
                    chunks = []
                    if not isctx:
                        kbase = t0 - 256
                        k0 = max(0, kbase)
                        k1 = min(n_in(l), t0 + (768 if N == 512 else 512))
                        nk = k1 - k0
                        P.op("sync", lambda e, k0=k0, nk=nk: e.dma_start(
                            out=kt[:, :, :nk], in_=SM.k[:, :, k0:k0 + nk].rearrange("h p t -> p h t")),
                            reads=dbufs(SM, "k", R8, k0, nk), writes=[b_kt], dsem=s_kt)
                        P.op("sync", lambda e, k0=k0, nk=nk: e.dma_start(
                            out=vt[:, :nk // 128, :], in_=SM.v[k0:k0 + nk, :].rearrange("(c p) f -> p c f", p=128)),
                            reads=dbufs(SM, "v", [0, 1], k0, nk), writes=[b_vt], dsem=s_vt)
                        for ci in range(nk // 128):
                            j = (k0 + ci * 128 - kbase) // 128
                            chunks.append(("loc", j, ci))
                    chunks += [("ctx", 0, 0), ("ctx", 1, 1)]
                    return chunks

                def emit_heads(S, t0, N, chunks, nxt_tile):
                    col = S.col
                    isctx = (S is SC)
                    for c in range(8):
                        P.op("vector", lambda e, c=c, N=N: e.scalar_tensor_tensor(
                            out=mix[:, 8 + c, :N], in0=ocv[:, c, :N], scalar=gcvs[:, l, c:c + 1], in1=rstdc_t[:, :N],
                            op0=ALU.mult, op1=ALU.mult), reads=[b_ocv[c], b_rstdc, b_const], writes=[b_mix])
                    edge = (not isctx) and t0 == 0
                    for h in range(8):
                        bt = bb = None
                        if edge:
                            bt, bb, bsem = bs.next()
                            P.op("sync", lambda e, bt=bt, h=h: e.dma_start(
                                out=bt[:].rearrange("p j q -> p (j q)"), in_=biasb[l][h]),
                                reads=[b_biasb[l][h]], writes=[bb], dsem=bsem)
                        pO, bO, _ = psO.next()
                        pD, bD, _ = psD.next()
                        def emitS(idx, h=h, bt=bt, bb=bb, N=N, edge=edge):
                            kind, j, ci = chunks[idx]
                            pS, bS, _ = psS.next()
                            if kind == "loc" and not edge:
                                def mm(e, pS=pS, h=h, ci=ci, j=j, N=N):
                                    e.matmul(pS[:, :N], lhsT=kt[:, h, ci * 128:(ci + 1) * 128], rhs=qt[:, h, :N],
                                             start=True, stop=False)
                                    na = N // 64
                                    ins = None
                                    for a in range(na):
                                        d0 = 2 * j - a
                                        bi = d0 + 1 if -1 <= d0 <= 8 else 10
                                        ins = e.matmul(pS[:, a * 64:(a + 1) * 64], lhsT=ident[:],
                                                       rhs=blkt[:, h, bi * 64:(bi + 1) * 64], start=False, stop=(a == na - 1))
                                    return ins
                                P.op("tensor", mm, reads=[b_kt, b_qt, b_blk, b_const], writes=[bS])
                            elif kind == "loc":
                                def mm(e, pS=pS, h=h, ci=ci, j=j, bt=bt, N=N):
                                    e.matmul(pS[:, :N], lhsT=kt[:, h, ci * 128:(ci + 1) * 128], rhs=qt[:, h, :N],
                                             start=True, stop=False)
                                    return e.matmul(pS[:, :N], lhsT=ident[:], rhs=bt[:, j, :N], start=False, stop=True)
                                P.op("tensor", mm, reads=[b_kt, b_qt, bb, b_const], writes=[bS])
                            else:
                                def mm(e, pS=pS, h=h, j=j, N=N):
                                    return e.matmul(pS[:, :N], lhsT=kctx[:, h, j * 128:(j + 1) * 128], rhs=qt[:, h, :N],
                                                    start=True, stop=True)
                                P.op("tensor", mm, reads=[b_ctxkv, b_qt], writes=[bS])
                            return pS, bS
                        nch = len(chunks)
                        nxt = emitS(0)
                        for idx, (kind, j, ci) in enumerate(chunks):
                            pS, bS = nxt
                            if idx + 1 < nch:
                                nxt = emitS(idx + 1)
                            vl = (vt, ci, b_vt) if kind == "loc" else (vctx, ci, b_ctxkv)
                            pt_, pbuf_, _ = ptl.next()
                            P.op("scalar", lambda e, pt_=pt_, pS=pS, N=N: e.activation(
                                out=pt_[:, :N], in_=pS[:, :N], func=AF.Exp), reads=[bS], writes=[pbuf_])
                            first = (idx == 0)
                            lastc = (idx == nch - 1)

                            def mm2(e, pt_=pt_, vl=vl, h=h, first=first, lastc=lastc, pO=pO, pD=pD, N=N):
                                e.matmul(pO[:, :N], lhsT=vl[0][:, vl[1], h * 128:(h + 1) * 128], rhs=pt_[:, :N],
                                         start=first, stop=lastc)
                                return e.matmul(pD[:, :N], lhsT=ones[:], rhs=pt_[:, :N], start=first, stop=lastc)
                            P.op("tensor", mm2, reads=[pbuf_, vl[2], b_const], writes=[bO, bD])
                        rt, rb, _ = rden.next()
                        P.op("vector", lambda e, rt=rt, pD=pD, N=N: e.reciprocal(out=rt[:, :N], in_=pD[:, :N]),
                             reads=[bD], writes=[rb])
                        P.op("vector", lambda e, rt=rt, pO=pO, h=h, N=N: e.tensor_tensor(
                            out=oo[:, h, :N], in0=pO[:, :N], in1=rt[:, :N], op=ALU.mult),
                            reads=[bO, rb], writes=[b_oo[h]])
                        P.op("vector", lambda e, h=h, N=N: e.tensor_tensor(out=sqb[:, h, :N], in0=oo[:, h, :N], in1=oo[:, h, :N],
                                                                        op=ALU.mult),
                             reads=[b_oo[h]], writes=[b_sqb])
                        if nxt_tile is not None:
                            if h == 0:
                                conv_load(nxt_tile[0], nxt_tile[1], nxt_tile[2], 0)
                            if h + 1 < 8:
                                conv_load(nxt_tile[0], nxt_tile[1], nxt_tile[2], h + 1)
                            conv_chunk(nxt_tile[0], nxt_tile[1], nxt_tile[2], h)
                    if nxt_tile is not None:
                        conv_stats(nxt_tile[2])

                cv_pending = {}

                def conv_load(S, t0, N, c):
                    z_, zb, zs = zt.next()
                    lo = max(0, t0 - 1)
                    hi = min(S.n, t0 + N + 1)
                    P.op("sync", lambda e, z_=z_, S=S, c=c, t0=t0, N=N: e.dma_start(
                        out=z_[:, :N + 2], in_=S.z[c, :, t0:t0 + N + 2]),
                        reads=dbufs(S, "z", [c], lo, hi - lo), writes=[zb], dsem=zs)
                    g_, gb, gs = bgt.next()
                    P.op("sync", lambda e, g_=g_, S=S, c=c, t0=t0, N=N: e.dma_start(
                        out=g_[:, :N], in_=S.bg[c, :, t0:t0 + N]), reads=dbufs(S, "bg", [c], t0, N),
                        writes=[gb], dsem=gs)
                    cv_pending[c] = (z_, zb, g_, gb)

                def conv_chunk(S, t0, N, c):
                    z_, zb, g_, gb = cv_pending.pop(c)
                    a_, ab, _ = acc.next()
                    P.op("vector", lambda e, a_=a_, z_=z_, c=c, N=N: e.tensor_scalar(
                        out=a_[:, :N], in0=z_[:, 1:N + 1], scalar1=cws[:, l, c, 1:2], scalar2=None, op0=ALU.mult),
                        reads=[zb, b_const], writes=[ab])
                    P.op("vector", lambda e, a_=a_, z_=z_, c=c, N=N: e.scalar_tensor_tensor(
                        out=a_[:, :N], in0=z_[:, 0:N], scalar=cws[:, l, c, 0:1], in1=a_[:, :N],
                        op0=ALU.mult, op1=ALU.add), reads=[zb, ab, b_const], writes=[ab])
                    P.op("vector", lambda e, a_=a_, z_=z_, c=c, N=N: e.scalar_tensor_tensor(
                        out=a_[:, :N], in0=z_[:, 2:N + 2], scalar=cws[:, l, c, 2:3], in1=a_[:, :N],
                        op0=ALU.mult, op1=ALU.add), reads=[zb, ab, b_const], writes=[ab])
                    P.op("vector", lambda e, a_=a_, g_=g_, N=N: e.tensor_tensor(
                        out=a_[:, :N], in0=a_[:, :N], in1=g_[:, :N], op=ALU.mult),
                        reads=[ab, gb], writes=[ab])
                    P.op("vector", lambda e, a_=a_, c=c, N=N: e.tensor_copy(out=ocv[:, c, :N], in_=a_[:, :N]),
                         reads=[ab], writes=[b_ocv[c]])
                    P.op("vector", lambda e, a_=a_, c=c, N=N: e.tensor_tensor(out=sqc[:, c, :N], in0=a_[:, :N], in1=a_[:, :N],
                                                                            op=ALU.mult),
                         reads=[ab], writes=[b_sqc])

                def conv_stats(N):
                    pG, bG, _ = psG.next()
                    stats_rstd(sqc, 0, 8, N, pG, bG, stdc_t, rstdc_t, b_sqc, b_rstdc, 1.0 / 1024)

                def emit_tail(S, t0, N, nxt):
                    col = S.col
                    isctx = (S is SC)
                    pG, bG, _ = psG.next()
                    stats_rstd(sqb, 0, 8, N, pG, bG, std_t, rstd_t, b_sqb, b_rstd, 1.0 / 1024)
                    for h in range(8):
                        P.op("vector", lambda e, h=h, N=N: e.scalar_tensor_tensor(
                            out=mix[:, h, :N], in0=oo[:, h, :N], scalar=gnas[:, l, h:h + 1], in1=rstd_t[:, :N],
                            op0=ALU.mult, op1=ALU.mult), reads=[b_oo[h], b_rstd, b_const], writes=[b_mix])
                    if dbg and not isctx:
                        P.op("gpsimd", lambda e, S=S, t0=t0, N=N: e.dma_start(
                            out=S.dmix[:, :, t0:t0 + N].rearrange("c p t -> p c t"), in_=mix[:, :, :N]),
                            reads=[b_mix], writes=[Buf()], dsem=s_qt)
                    for f in range(KC):
                        wt_, wbuf, wsem = wb.next()
                        P.op("sync", lambda e, wt_=wt_, u=U_O + f: e.dma_start(out=wt_[:], in_=wtb[l][u]),
                             reads=[b_wtb[l][U_O + f]], writes=[wbuf], dsem=wsem)
                        x_, xb_, xs_ = xc.next()
                        P.op("sync", lambda e, x_=x_, S=S, f=f, t0=t0, N=N: e.dma_start(
                            out=x_[:, :N], in_=S.x[f, :, t0:t0 + N]), reads=dbufs(S, "x", [f], t0, N),
                            writes=[xb_], dsem=xs_)
                        pG, bG, _ = psG.next()

                        def mm(e, wt_=wt_, pG=pG, N=N):
                            ins = None
                            for k in range(KC):
                                ins = e.matmul(pG[:, :N], lhsT=wt_[:, k * 128:(k + 1) * 128], rhs=mix[:, k, :N],
                                               start=(k == 0), stop=(k == KC - 1))
                            return ins
                        P.op("tensor", mm, reads=[wbuf, b_mix], writes=[bG])
                        P.op("vector", lambda e, pG=pG, x_=x_, f=f, N=N, col=col: e.scalar_tensor_tensor(
                            out=oo[:, f, :N], in0=pG[:, :N], scalar=mods[:, l, 32 + f, col:col + 1], in1=x_[:, :N],
                            op0=ALU.mult, op1=ALU.add), reads=[bG, xb_, b_const, b_mods[l]], writes=[b_oo[f]])
                        P.op("gpsimd", lambda e, S=S, f=f, t0=t0, N=N: e.dma_start(
                            out=S.x[f, :, t0:t0 + N], in_=oo[:, f, :N]),
                            reads=[b_oo[f]], writes=dbufs(S, "x", [f], t0, N), dsem=xs_sem[f % 4])
                        P.op("scalar", lambda e, f=f, N=N: e.activation(out=sqb[:, f, :N], in_=oo[:, f, :N], func=AF.Square),
                             reads=[b_oo[f]], writes=[b_sqb])
                        if dbg and not isctx:
                            P.op("gpsimd", lambda e, S=S, f=f, t0=t0, N=N: e.dma_start(
                                out=S.dxn[f, :, t0:t0 + N], in_=oo[:, f, :N]),
                                reads=[b_oo[f]], writes=[Buf()], dsem=xs_sem[f % 4])
                    pG, bG, _ = psG.next()
                    stats_rstd(sqb, 0, KC, N, pG, bG, std_t, rstd_t, b_sqb, b_rstd, 1.0 / D)
                    for c in range(KC):
                        tt, tb, _ = tmp.next()
                        P.op("vector", lambda e, c=c, tt=tt, N=N, col=col: e.scalar_tensor_tensor(
                            out=tt[:, :N], in0=oo[:, c, :N], scalar=A2[:, l, c, col:col + 1], in1=rstd_t[:, :N],
                            op0=ALU.mult, op1=ALU.mult), reads=[b_oo[c], b_rstd, b_const, b_mods[l]], writes=[tb])
                        hs_, hb_, hsem = hst.next()
                        P.op("scalar", lambda e, c=c, tt=tt, hs_=hs_, N=N, col=col: e.activation(
                            out=hs_[:, :N], in_=tt[:, :N], func=AF.Identity, bias=mods[:, l, 48 + c, col:col + 1],
                            scale=1.0), reads=[tb, b_const, b_mods[l]], writes=[hb_])
                        P.op("gpsimd", lambda e, hs_=hs_, S=S, c=c, t0=t0, N=N: e.dma_start(
                            out=S.h2[c, :, t0:t0 + N], in_=hs_[:, :N]),
                            reads=[hb_], writes=dbufs(S, "h2", [c], t0, N), dsem=hsem)

                tlist = [(S, t0, N) for (S, tl) in streams_B for (t0, N) in tl]
                ld = emit_loads(*tlist[0])
                for c in range(8):
                    conv_load(tlist[0][0], tlist[0][1], tlist[0][2], c)
                    conv_chunk(tlist[0][0], tlist[0][1], tlist[0][2], c)
                conv_stats(tlist[0][2])
                for ti_, (S, t0, N) in enumerate(tlist):
                    chunks = ld
                    nxt = tlist[ti_ + 1] if ti_ + 1 < len(tlist) else None
                    emit_heads(S, t0, N, chunks, nxt)
                    if nxt is not None:
                        ld = emit_loads(*nxt)
                    emit_tail(S, t0, N, nxt)
                    pump(7)
                P.end_phase()

            def phaseC(pc):
                def sbc(name, shape, dt=F32):
                    return pc.enter_context(_sb(name, shape, dt))
                h2t = sbc("h2c", [128, KC, 512], BF16)
                b_h2t = Buf()
                s_h2t = P.dsem()
                hid = sbc("hid", [128, 64, 512], BF16)
                b_hid = [Buf() for _ in range(64)]
                rr = Rot(P, pc, nc, "rr", 2, [128, 512], F32, sem=False)
                xc = Rot(P, pc, nc, "xc", 3, [128, 512], F32)
                xs_sem = [P.dsem() for _ in range(3)]
                wb = Rot(P, pc, nc, "wbc", 3, [128, 2048], BF16)
                w2 = Rot(P, pc, nc, "w2c", 2, [128, 8192], BF16)
                pb = Rot(P, pc, nc, "pbc", 4, [128, 512], F32, psum=True)
                bg_steps = []
                bg_pend = []

                def bg_point():
                    if bg_pend:
                        bg_pend.pop(0)()
                    if bg_steps:
                        ld_, cp_ = bg_steps.pop(0)
                        ld_()
                        if cp_ is not None:
                            bg_pend.append(cp_)
                if l + 1 < depth:
                    wa = Rot(P, pc, nc, "wab", 6, [128, KC, 256], BF16)
                    pm = pc.enter_context(_ps("pmb", [128, 96, 2], F32))
                    bg_steps = mods_steps(l + 1, wa, pm, Buf())
                if last:
                    xo = sbc("xo", [128, KC, 512])
                    b_xo = Buf()
                    sqf = sbc("sqf", [128, KC, 512], BF16)
                    b_sqf = Buf()
                    std_t = sbc("stdc", [128, 512])
                    rstd_t = sbc("rstdc", [128, 512])
                    b_rstd = Buf()
                    pst = pc.enter_context(_ps("pstc", [128, 512], F32))
                    b_pst = Buf()
                    ost = Rot(P, pc, nc, "ost", 3, [128, 512], F32)
                def load_h2(S, t0, N):
                    P.op("sync", lambda e: e.dma_start(
                        out=h2t[:, :, :N], in_=S.h2[:, :, t0:t0 + N].rearrange("c p t -> p c t")),
                        reads=dbufs(S, "h2", R16, t0, N), writes=[b_h2t], dsem=s_h2t)
                tlistC = [(S, t0, N) for (S, tl) in streams_B for (t0, N) in tl]
                load_h2(*tlistC[0])
                for tci_, (S, t0, N) in enumerate(tlistC):
                    col = S.col
                    if True:
                        for _ in range(5):
                            if bg_steps:
                                ld_, cp_ = bg_steps.pop(0)
                                if cp_ is None:
                                    bg_pend.append(ld_)
                                else:
                                    ld_()
                                    bg_pend.append(cp_)
                        for f in range(64):
                            wt_, wbuf, wsem = wb.next()
                            P.op("sync", lambda e, wt_=wt_, u=U_M1 + f: e.dma_start(out=wt_[:], in_=wtb[l][u]),
                                 reads=[b_wtb[l][U_M1 + f]], writes=[wbuf], dsem=wsem)
                            pt, pbuf, _ = pb.next()

                            def mm(e, wt_=wt_, pt=pt, N=N):
                                ins = None
                                for k in range(KC):
                                    ins = e.matmul(pt[:, :N], lhsT=wt_[:, k * 128:(k + 1) * 128], rhs=h2t[:, k, :N],
                                                   start=(k == 0), stop=(k == KC - 1))
                                return ins
                            P.op("tensor", mm, reads=[wbuf, b_h2t], writes=[pbuf])
                            r_, rb, _ = rr.next()
                            P.op("scalar", lambda e, r_=r_, pt=pt, N=N: e.activation(out=r_[:, :N], in_=pt[:, :N], func=AF.Relu),
                                 reads=[pbuf], writes=[rb])
                            P.op("vector", lambda e, r_=r_, f=f, N=N: e.tensor_tensor(
                                out=hid[:, f, :N], in0=r_[:, :N], in1=r_[:, :N], op=ALU.mult), reads=[rb], writes=[b_hid[f]])
                        for f in range(KC):
                            wt_, wbuf, wsem = w2.next()
                            P.op("sync", lambda e, wt_=wt_, f=f: e.dma_start(
                                out=wt_[:].rearrange("p (u e) -> p u e", u=4),
                                in_=wtb[l][U_M2 + 4 * f:U_M2 + 4 * f + 4].rearrange("u p e -> p u e")),
                                reads=b_wtb[l][U_M2 + 4 * f:U_M2 + 4 * f + 4], writes=[wbuf], dsem=wsem)
                            if f == 1 and tci_ + 1 < len(tlistC):
                                load_h2(*tlistC[tci_ + 1])
                            x_, xb_, xs_ = xc.next()
                            P.op("sync", lambda e, x_=x_, S=S, f=f, t0=t0, N=N: e.dma_start(
                                out=x_[:, :N], in_=S.x[f, :, t0:t0 + N]), reads=dbufs(S, "x", [f], t0, N),
                                writes=[xb_], dsem=xs_)
                            pt, pbuf, _ = pb.next()

                            def mm(e, wt_=wt_, pt=pt, N=N):
                                ins = None
                                for k in range(64):
                                    ins = e.matmul(pt[:, :N], lhsT=wt_[:, k * 128:(k + 1) * 128], rhs=hid[:, k, :N],
                                                   start=(k == 0), stop=(k == 63))
                                return ins
                            P.op("tensor", mm, reads=[wbuf] + b_hid, writes=[pbuf])
                            if not last:
                                P.op("vector", lambda e, x_=x_, pt=pt, f=f, N=N, col=col: e.scalar_tensor_tensor(
                                    out=x_[:, :N], in0=pt[:, :N], scalar=mods[:, l, 80 + f, col:col + 1], in1=x_[:, :N],
                                    op0=ALU.mult, op1=ALU.add), reads=[pbuf, xb_, b_const, b_mods[l]], writes=[xb_])
                                P.op("gpsimd", lambda e, x_=x_, S=S, f=f, t0=t0, N=N: e.dma_start(
                                    out=S.x[f, :, t0:t0 + N], in_=x_[:, :N]), reads=[xb_],
                                    writes=dbufs(S, "x", [f], t0, N), dsem=xs_sem[f % 3])
                            else:
                                P.op("vector", lambda e, x_=x_, pt=pt, f=f, N=N, col=col: e.scalar_tensor_tensor(
                                    out=xo[:, f, :N], in0=pt[:, :N], scalar=mods[:, l, 80 + f, col:col + 1], in1=x_[:, :N],
                                    op0=ALU.mult, op1=ALU.add), reads=[pbuf, xb_, b_const, b_mods[l]], writes=[b_xo])
                        if last:
                            for c4 in range(4):
                                P.op("scalar", lambda e, c4=c4, N=N: e.activation(
                                    out=sqf[:, c4 * 4:(c4 + 1) * 4, :N], in_=xo[:, c4 * 4:(c4 + 1) * 4, :N], func=AF.Square),
                                    reads=[b_xo], writes=[b_sqf])
                            stats_rstd(sqf, 0, KC, N, pst, b_pst, std_t, rstd_t, b_sqf, b_rstd, 1.0 / D)
                            for c in range(KC):
                                o_, ob, os_ = ost.next()
                                P.op("vector", lambda e, c=c, o_=o_, N=N: e.scalar_tensor_tensor(
                                    out=o_[:, :N], in0=xo[:, c, :N], scalar=gfs[:, c:c + 1], in1=rstd_t[:, :N],
                                    op0=ALU.mult, op1=ALU.mult), reads=[b_xo, b_rstd, b_const], writes=[ob])
                                P.op("gpsimd", lambda e, o_=o_, c=c, t0=t0, N=N: e.dma_start(
                                    out=outT[c, :, t0:t0 + N], in_=o_[:, :N]),
                                    reads=[ob], writes=[Buf()], dsem=os_)
                        pump(7)
                        while bg_pend:
                            bg_pend.pop(0)()
                while bg_steps:
                    ld_, cp_ = bg_steps.pop(0)
                    ld_()
                    if cp_ is not None:
                        cp_()
                pump(len(pending))
                P.end_phase()
            for ph in (phaseA, phaseB, phaseC):
                with ExitStack() as st_:
                    ph(st_)

        for l_ in range(depth):
            layer(l_)
        P.emit()
    return nc


def _fm_units(W):
    K, n = W.shape
    kc = K // 128
    nb = n // 128
    a = W.reshape(kc, 128, nb, 128).transpose(2, 1, 0, 3).reshape(nb, 128, kc * 128)
    if kc == 16:
        return a
    u = kc // 16
    return a.reshape(nb, 128, u, 2048).transpose(0, 2, 1, 3).reshape(nb * u, 128, 2048)


def _v_units(Wv):
    a = Wv.reshape(16, 128, 2, 512).transpose(2, 1, 0, 3).reshape(2, 128, 8192)
    return a.reshape(2, 128, 4, 2048).transpose(0, 2, 1, 3).reshape(8, 128, 2048)


def _build_bias(rpb, rev):
    res = []
    for ti in (0, 2):
        lq = ti * 512 + np.arange(512)
        lk = ti * 512 - 256 + np.arange(1024)
        gq = (SEQ - 1 - lq) if rev else lq
        gk = (SEQ - 1 - lk) if rev else lk
        rq, cq = gq // 64, gq % 64
        rk, ck = gk // 64, gk % 64
        rs = np.clip(rq - 4, 0, 120)
        cs = np.clip(cq - 8, 0, 48)
        valid = ((lk >= 0)[:, None] & (rk[:, None] >= rs[None]) & (rk[:, None] < rs[None] + 8)
                 & (ck[:, None] >= cs[None]) & (ck[:, None] < cs[None] + 16))
        dr = np.clip(rk[:, None] - rq[None] + 7, 0, 14)
        dc = np.clip(ck[:, None] - cq[None] + 15, 0, 30)
        vals = rpb[:, :, dr, dc]
        res.append(np.where(valid[None, None], vals, np.float32(NEG)).astype(np.float32))
    edge = res[0].reshape(DEPTH, 8, 8, 128, 512).transpose(0, 1, 3, 2, 4).reshape(DEPTH, 8, 128, 4096)
    blk = np.full((DEPTH, 8, 128, 11, 64), np.float32(NEG), np.float32)
    for d0 in range(-1, 9):
        a = d0 % 2
        j = (d0 + a) // 2
        blk[:, :, :, d0 + 1, :] = res[1][:, :, j * 128:(j + 1) * 128, a * 64:(a + 1) * 64]
    return np.ascontiguousarray(edge), np.ascontiguousarray(blk.reshape(DEPTH, 8, 128, 704)), res[1]


def _fm(v, nchunk):
    sh = v.shape[:-1]
    a = v.reshape(sh + (nchunk, 128))
    return np.ascontiguousarray(np.moveaxis(a, -1, 0))


_NC_CACHE = {}


def _prep(x, c, ctx, c_ctx, w_ada, b_ada, g_norm1, w_in, rpb, conv_w, g_na_out, g_conv_out,
          w_out, g_norm2, w_mlp1, w_mlp2, g_final):
    f = lambda a: np.asarray(a, dtype=np.float32)
    x, c, ctx, c_ctx, w_ada, b_ada, g_norm1, w_in, rpb, conv_w = map(f, (x, c, ctx, c_ctx, w_ada, b_ada, g_norm1, w_in, rpb, conv_w))
    g_na_out, g_conv_out, w_out, g_norm2, w_mlp1, w_mlp2, g_final = map(f, (g_na_out, g_conv_out, w_out, g_norm2, w_mlp1, w_mlp2, g_final))
    wts = np.empty((DEPTH, NU, 128, 2048), np.float32)
    for l in range(DEPTH):
        wi = w_in[l]
        wts[l, U_Q:U_Q + 8] = _fm_units(wi[:, 0:1024])
        wts[l, U_K:U_K + 8] = _fm_units(wi[:, 1024:2048])
        wts[l, U_V:U_V + 8] = _v_units(wi[:, 2048:3072])
        wts[l, U_BG:U_BG + 8] = _fm_units(wi[:, 3072:4096])
        wts[l, U_CG:U_CG + 8] = _fm_units(wi[:, 4096:5120])
        wts[l, U_U:U_U + 8] = _fm_units(wi[:, 5120:6144])
        wts[l, U_O:U_O + 16] = _fm_units(w_out[l])
        wts[l, U_M1:U_M1 + 64] = _fm_units(w_mlp1[l])
        wts[l, U_M2:U_M2 + 64] = _fm_units(w_mlp2[l])
    shared = {
        "w_ada": np.ascontiguousarray(w_ada),
        "b_ada": _fm(b_ada, 96),
        "g1": _fm(g_norm1, KC), "g2": _fm(g_norm2, KC), "gf": _fm(g_final, KC),
        "gna": _fm(g_na_out, 8), "gcv": _fm(g_conv_out, 8),
        "wts": wts,
        "ident": np.eye(128, dtype=np.float32),
    }
    cwf = _fm(conv_w, 8)
    cw_fwd = np.ascontiguousarray(cwf.transpose(0, 1, 3, 2))
    cw_rev = np.ascontiguousarray(cw_fwd[:, :, :, ::-1])
    bias_fwd, blk_fwd, _ = _build_bias(rpb, False)
    bias_rev, blk_rev, _ = _build_bias(rpb, True)
    ccx = _fm(c_ctx, KC)
    in_maps = []
    gidxs = []
    for core in range(8):
        b, rev = core // 2, core % 2
        gidx = (SEQ - 1 - np.arange(NT)) if rev else np.arange(NT)
        gidxs.append(gidx)
        xT = np.ascontiguousarray(x[b][gidx].T).reshape(KC, 128, NT)
        cb = ctx[b][::-1] if rev else ctx[b]
        cT = np.ascontiguousarray(cb.T).reshape(KC, 128, CTXL)
        cc = np.ascontiguousarray(np.stack([_fm(c[b], KC), ccx], axis=-1))
        m = dict(shared)
        m.update({"xT": xT, "ctxT": cT, "cc": cc, "cw": cw_rev if rev else cw_fwd,
                  "bias": bias_rev if rev else bias_fwd, "blk": blk_rev if rev else blk_fwd})
        in_maps.append(m)
    return in_maps, gidxs


def kernel(x, c, ctx, c_ctx, w_ada, b_ada, g_norm1, w_in, rpb, conv_w, g_na_out, g_conv_out,
           w_out, g_norm2, w_mlp1, w_mlp2, g_final):
    in_maps, gidxs = _prep(x, c, ctx, c_ctx, w_ada, b_ada, g_norm1, w_in, rpb, conv_w, g_na_out, g_conv_out,
                           w_out, g_norm2, w_mlp1, w_mlp2, g_final)
    if "nc" not in _NC_CACHE:
        _NC_CACHE["nc"] = build()
    res = run_bass_kernel_spmd(_NC_CACHE["nc"], in_maps, core_ids=list(range(8)))
    out = np.empty((4, SEQ, D), np.float32)
    for core in range(8):
        b = core // 2
        o = np.asarray(res.results[core]["outT"]).reshape(D, OWN)
        out[b, gidxs[core][:OWN], :] = o.T
    return out
```
